# Optimizing a Trainium2 kernel written in Bass

```python
import jax, jax.numpy as jnp
from jax import lax
import numpy as np


D_MODEL = 4096
BATCH = 2
SEQ = 4096
DEPTH = 2
DEC_BATCH = 8
DEC_SEQ = 16
PAST_LEN = 1024

CHUNK = 64
N_MIXERS = 2
N_POOL_LAYERS = (DEPTH + 1) // 2
N_CONV_LAYERS = DEPTH // 2
POOL_WINDOWS = (2, 4, 8, 16)
N_POOL_GROUPS = len(POOL_WINDOWS)
POOL_GROUP = D_MODEL // N_POOL_GROUPS
POOL_HIST = max(POOL_WINDOWS) - 1
CONV_WIDTH = 31
CONV_HIST = CONV_WIDTH - 1
D_FF = 256 * ((8 * D_MODEL // 3 + 255) // 256)
PLE_DIM = 256
EPS = 1e-6

kernel_name = 'streaming_pool_conformer_macaron_ple'


def _rmsnorm(x, g):
    xf = x.astype(jnp.float32)
    r = lax.rsqrt(jnp.mean(xf * xf, axis=-1, keepdims=True) + EPS)
    return (xf * r).astype(x.dtype) * g


def _layernorm(x, g, b):
    xf = x.astype(jnp.float32)
    mu = jnp.mean(xf, axis=-1, keepdims=True)
    var = jnp.mean(jnp.square(xf - mu), axis=-1, keepdims=True)
    return ((xf - mu) * lax.rsqrt(var + EPS)).astype(x.dtype) * g + b


def _swiglu(h, w_gate, w_up, w_down):
    return (jax.nn.silu(h @ w_gate) * (h @ w_up)) @ w_down


def _pool_mixer(h, hist, start_pos, w, b, scale):
    B, n, _ = h.shape
    h_ext = jnp.concatenate([hist, h], axis=1)
    cs = jnp.cumsum(h_ext.astype(jnp.float32), axis=1)
    cs = jnp.pad(cs, ((0, 0), (1, 0), (0, 0)))
    pos = start_pos + jnp.arange(n)
    means = []
    for g, win in enumerate(POOL_WINDOWS):
        sl = slice(g * POOL_GROUP, (g + 1) * POOL_GROUP)
        lo = POOL_HIST + 1 - win
        s = cs[:, POOL_HIST + 1:, sl] - cs[:, lo:lo + n, sl]
        cnt = jnp.minimum(pos + 1, win).astype(jnp.float32)
        means.append(s / cnt[None, :, None])
    mean = jnp.concatenate(means, axis=-1).astype(h.dtype)
    d = (mean - h).reshape(B, n, N_POOL_GROUPS, POOL_GROUP)
    y = jnp.einsum('bngc,gcd->bngd', d, w) + b
    y = y.reshape(B, n, D_MODEL) * scale
    return y, h_ext[:, -POOL_HIST:]


def _conv_module(h, hist, w_pw1, b_pw1, w_dw, b_dw, ln_g, ln_b, w_pw2, b_pw2):
    a = h @ w_pw1 + b_pw1
    u = a[..., :D_MODEL] * jax.nn.sigmoid(a[..., D_MODEL:])
    u_ext = jnp.concatenate([hist, u], axis=1)
    c = lax.conv_general_dilated(
        u_ext, w_dw[:, None, :], window_strides=(1,), padding='VALID',
        dimension_numbers=('NWC', 'WIO', 'NWC'), feature_group_count=D_MODEL) + b_dw
    c = jax.nn.silu(_layernorm(c, ln_g, ln_b))
    y = c @ w_pw2 + b_pw2
    return y, u_ext[:, -CONV_HIST:]


def _trunk(x, p, pool_hist, conv_hist, start_pos, W):
    new_pool, new_conv = [], []
    for i in range(DEPTH):
        h = _rmsnorm(x, W['g_ffn1'][i])
        x = x + 0.5 * _swiglu(h, W['ffn1_w_gate'][i], W['ffn1_w_up'][i], W['ffn1_w_down'][i])
        h = _rmsnorm(x, W['g_mix'][i])
        j = i // N_MIXERS
        if i % N_MIXERS == 0:
            y, st = _pool_mixer(h, pool_hist[j], start_pos, W['pool_w'][j], W['pool_b'][j], W['pool_scale'][j])
            new_pool.append(st)
        else:
            y, st = _conv_module(h, conv_hist[j], W['conv_w_pw1'][j], W['conv_b_pw1'][j], W['conv_w_dw'][j],
                                 W['conv_b_dw'][j], W['conv_ln_g'][j], W['conv_ln_b'][j],
                                 W['conv_w_pw2'][j], W['conv_b_pw2'][j])
            new_conv.append(st)
        x = x + y
        h = _rmsnorm(x, W['g_ffn2'][i])
        x = x + 0.5 * _swiglu(h, W['ffn2_w_gate'][i], W['ffn2_w_up'][i], W['ffn2_w_down'][i])
        gate = jax.nn.sigmoid(_rmsnorm(x, W['g_ple'][i]) @ W['ple_w_gate'][i])
        x = x + (p[i] @ W['ple_w_proj'][i]) * gate
    return _rmsnorm(x, W['g_final']), jnp.stack(new_pool), jnp.stack(new_conv)


def setup_inputs(seed: int = 0) -> dict:
    key = jax.random.key(seed)
    ks = iter(jax.random.split(key, 40))
    nrm = lambda shape, s: jax.random.normal(next(ks), shape, jnp.float32) * s
    gain = lambda shape: 1.0 + 0.05 * jax.random.normal(next(ks), shape, jnp.float32)
    D, F, G = D_MODEL, D_FF, POOL_GROUP
    NP, NC = N_POOL_LAYERS, N_CONV_LAYERS
    return {
        'x_prompt': nrm((BATCH, SEQ, D), 1.0),
        'x_sample': nrm((DEC_BATCH, DEC_SEQ, D), 1.0),
        'state_pool': nrm((NP, DEC_BATCH, POOL_HIST, D), 1.0),
        'state_conv': nrm((NC, DEC_BATCH, CONV_HIST, D), 0.5),
        'p_prompt': nrm((DEPTH, BATCH, SEQ, PLE_DIM), 1.0),
        'p_sample': nrm((DEPTH, DEC_BATCH, DEC_SEQ, PLE_DIM), 1.0),
        'g_ffn1': gain((DEPTH, D)),
        'ffn1_w_gate': nrm((DEPTH, D, F), D ** -0.5),
        'ffn1_w_up': nrm((DEPTH, D, F), D ** -0.5),
        'ffn1_w_down': nrm((DEPTH, F, D), F ** -0.5),
        'g_mix': gain((DEPTH, D)),
        'pool_w': nrm((NP, N_POOL_GROUPS, G, G), G ** -0.5),
        'pool_b': nrm((NP, N_POOL_GROUPS, G), 0.02),
        'pool_scale': gain((NP, D)),
        'conv_w_pw1': nrm((NC, D, 2 * D), D ** -0.5),
        'conv_b_pw1': nrm((NC, 2 * D), 0.02),
        'conv_w_dw': nrm((NC, CONV_WIDTH, D), CONV_WIDTH ** -0.5),
        'conv_b_dw': nrm((NC, D), 0.02),
        'conv_ln_g': gain((NC, D)),
        'conv_ln_b': nrm((NC, D), 0.02),
        'conv_w_pw2': nrm((NC, D, D), D ** -0.5),
        'conv_b_pw2': nrm((NC, D), 0.02),
        'g_ffn2': gain((DEPTH, D)),
        'ffn2_w_gate': nrm((DEPTH, D, F), D ** -0.5),
        'ffn2_w_up': nrm((DEPTH, D, F), D ** -0.5),
        'ffn2_w_down': nrm((DEPTH, F, D), F ** -0.5),
        'g_ple': gain((DEPTH, D)),
        'ple_w_gate': nrm((DEPTH, D, D), D ** -0.5),
        'ple_w_proj': nrm((DEPTH, PLE_DIM, D), PLE_DIM ** -0.5),
        'g_final': gain((D,)),
    }


def reference(x_prompt, x_sample, state_pool, state_conv, p_prompt, p_sample,
              g_ffn1, ffn1_w_gate, ffn1_w_up, ffn1_w_down, g_mix,
              pool_w, pool_b, pool_scale,
              conv_w_pw1, conv_b_pw1, conv_w_dw, conv_b_dw, conv_ln_g, conv_ln_b, conv_w_pw2, conv_b_pw2,
              g_ffn2, ffn2_w_gate, ffn2_w_up, ffn2_w_down, g_ple, ple_w_gate, ple_w_proj, g_final):
    W = {
        'g_ffn1': g_ffn1, 'ffn1_w_gate': ffn1_w_gate, 'ffn1_w_up': ffn1_w_up, 'ffn1_w_down': ffn1_w_down,
        'g_mix': g_mix, 'pool_w': pool_w, 'pool_b': pool_b, 'pool_scale': pool_scale,
        'conv_w_pw1': conv_w_pw1, 'conv_b_pw1': conv_b_pw1, 'conv_w_dw': conv_w_dw, 'conv_b_dw': conv_b_dw,
        'conv_ln_g': conv_ln_g, 'conv_ln_b': conv_ln_b, 'conv_w_pw2': conv_w_pw2, 'conv_b_pw2': conv_b_pw2,
        'g_ffn2': g_ffn2, 'ffn2_w_gate': ffn2_w_gate, 'ffn2_w_up': ffn2_w_up, 'ffn2_w_down': ffn2_w_down,
        'g_ple': g_ple, 'ple_w_gate': ple_w_gate, 'ple_w_proj': ple_w_proj, 'g_final': g_final,
    }
    B = x_prompt.shape[0]
    zero_pool = jnp.zeros((N_POOL_LAYERS, B, POOL_HIST, D_MODEL), x_prompt.dtype)
    zero_conv = jnp.zeros((N_CONV_LAYERS, B, CONV_HIST, D_MODEL), x_prompt.dtype)
    y_prompt, new_pool_prompt, new_conv_prompt = _trunk(x_prompt, p_prompt, zero_pool, zero_conv, 0, W)
    y_sample, new_pool_sample, new_conv_sample = _trunk(x_sample, p_sample, state_pool, state_conv, PAST_LEN, W)
    return (y_prompt, y_sample, new_pool_prompt, new_conv_prompt, new_pool_sample, new_conv_sample)
```

```python
import numpy as np
import contextlib
import concourse.bass as bass
import concourse.mybir as mybir
from concourse.bass_utils import run_bass_kernel_spmd

F32 = mybir.dt.float32
BF16 = mybir.dt.bfloat16
AF = mybir.ActivationFunctionType
ALU = mybir.AluOpType

N_CORES = 8
POOL_WINDOWS = (2, 4, 8, 16)
HP = 15
HC = 30
CW = 31
HALO = 45


class Cfg:
    def __init__(self, D=4096, F=11008, PLE=256, SEQ=4096, BATCH=2, DEC_BATCH=8, DEC_SEQ=16,
                 PAST_LEN=1024, T=544, OWN1=496, NST=7, G=4):
        self.D, self.F, self.PLE, self.SEQ, self.BATCH = D, F, PLE, SEQ, BATCH
        self.DEC_BATCH, self.DEC_SEQ, self.PAST_LEN = DEC_BATCH, DEC_SEQ, PAST_LEN
        self.T, self.OWN1, self.NST, self.G = T, OWN1, NST, G
        self.KD, self.KF, self.KP = D // 128, F // 128, PLE // 128
        assert D % 512 == 0 and F % 128 == 0 and PLE % 128 == 0
        self.KG = self.KD // 4
        self.CPB = N_CORES // BATCH
        self.NP = SEQ // self.CPB
        self.OWN2 = T - DEC_SEQ
        assert self.OWN1 + self.OWN2 == self.NP, (self.OWN1, self.OWN2, self.NP)
        self.PAD1 = T - HALO - OWN1
        assert self.PAD1 >= 0 and T % 2 == 0
        assert DEC_BATCH == N_CORES
        self.H = T // 2
        self.SW = max((self.KD + self.KP) * 128, CW * 128)
        self.EPS = 1e-6
        KD = self.KD
        off = {}
        o = 0
        for l in range(2):
            for nm in ("g_ffn1", "g_mix", "g_ffn2", "g_ple"):
                off[(nm, l)] = o
                o += KD
        for nm, w in (("g_final", KD), ("pool_b", KD), ("pool_scale", KD), ("b_pw1", 2 * KD), ("b_dw", KD),
                      ("ln_g", KD), ("ln_b", KD), ("b_pw2", KD), ("mask", 1)):
            off[nm] = o
            o += w
        self.voff = off
        self.NV = o
        self.segs = [[(0, T)], [(0, self.OWN2), (self.OWN2, DEC_SEQ)]]
        self.real_end0 = HALO + OWN1


def stage_plan(cfg):
    plan = []
    KD, KF, G = cfg.KD, cfg.KF, cfg.G
    for l in range(2):
        def ffn(which):
            for g0 in range(0, KF, G):
                grp = list(range(g0, min(g0 + G, KF)))
                for j in grp:
                    plan.append(("g", l, which, j))
                    plan.append(("u", l, which, j))
                for j in grp:
                    plan.append(("d", l, which, j))
        ffn(1)
        if l == 0:
            for g in range(4):
                for j in range(cfg.KG):
                    plan.append(("pool", g, j))
        else:
            for c in range(KD):
                plan.append(("pw1", c))
                plan.append(("pw1", KD + c))
                if c >= 1:
                    plan.append(("dw", c - 1))
            plan.append(("dw", KD - 1))
            for c in range(KD):
                plan.append(("pw2", c))
        ffn(2)
        for c in range(KD):
            plan.append(("ple", l, c))
    return plan


class Buf:
    __slots__ = ("w", "r", "name")

    def __init__(self, name=""):
        self.w = None
        self.r = {}
        self.name = name


ENG_SEM = {"pe": "pe", "act": "act", "dve": "dve"}


class Sched:
    def __init__(self):
        self.ops = {e: [] for e in ("pe", "act", "dve", "pool", "sp")}
        self.count = {}
        self.waited = {e: {} for e in self.ops}
        self.fence = {}

    def set_fence(self):
        self.fence = {k: self.count.get(k, 0) for k in ("pe", "act", "dve")}

    def op(self, eng, fn, reads=(), writes=(), sem=None, inc=1, deps=()):
        need = {}

        def add(k, v):
            if v and need.get(k, 0) < v:
                need[k] = v

        for b in reads:
            if b.w is not None:
                add(*b.w)
        for b in writes:
            if b.w is not None:
                add(*b.w)
            for k, v in b.r.items():
                add(k, v)
        for t in deps:
            if t is not None:
                add(*t)
        if eng in ("pe", "act", "dve"):
            for k, v in self.fence.items():
                add(k, v)
        if eng == "pe":
            need.pop("pe", None)
        wd = self.waited[eng]
        waits = []
        for k, v in need.items():
            if wd.get(k, 0) < v:
                wd[k] = v
                waits.append((k, v))
        semk = sem or ENG_SEM[eng]
        self.count[semk] = self.count.get(semk, 0) + inc
        tok = (semk, self.count[semk])
        self.ops[eng].append((waits, fn, (semk, inc)))
        for b in reads:
            if b.r.get(semk, 0) < tok[1]:
                b.r[semk] = tok[1]
        for b in writes:
            b.w = tok
            b.r = {}
        return tok

    def emit(self, eng, e, sems):
        for waits, fn, (semk, inc) in self.ops[eng]:
            for k, v in waits:
                e.wait_ge(sems[k], v)
            ins = fn(e)
            ins.then_inc(sems[semk], inc)


def build_program(cfg):
    D, KD, KF, KP, KG, T, H, G, NST, SW = cfg.D, cfg.KD, cfg.KF, cfg.KP, cfg.KG, cfg.T, cfg.H, cfg.G, cfg.NST, cfg.SW
    EPS = cfg.EPS
    plan = stage_plan(cfg)
    NS = len(plan)
    nc = bass.Bass("TRN2", target_bir_lowering=False)

    def din(name, shape):
        return nc.dram_tensor(name, list(shape), F32, kind="ExternalInput").ap()

    def dout(name, shape):
        return nc.dram_tensor(name, list(shape), F32, kind="ExternalOutput").ap()

    xin = din("xin", [2, D, T])
    pin = din("pin", [2 * 2, KP * 128, T])
    invc = din("invc", [2, 128, 4 * T])
    spool = din("spool", [D, HP])
    sconv = din("sconv", [D, HC])
    vecs_d = din("vecs", [128, cfg.NV])
    w_all = din("w_all", [NS, 128, SW])
    yout = dout("yout", [2, D, T])
    pool_out = dout("pool_out", [2, D, HP])
    conv_out = dout("conv_out", [2, D, HC])
    xspill = dout("xspill", [D, T])

    NTMP = 4
    NEP = T + 2 * HP
    NEC = T + 2 * HC
    r2_ffn = G * T
    r2_pool = 4 * T + 4 * NEP
    r2_conv = 3 * NEC + 2 * T
    R2 = max(r2_ffn, r2_pool, r2_conv)
    SCR = NTMP * T + R2

    S = Sched()
    with contextlib.ExitStack() as es:
        def sb(name, shape, dt):
            return es.enter_context(nc.sbuf_tensor("sb_" + name, list(shape), dt))

        xT = sb("xT", [128, KD, T], F32)
        hb = sb("hb", [128, KD, T], BF16)
        ring = sb("ring", [128, NST, SW], BF16)
        scr = sb("scr", [128, SCR], F32)
        rbc = sb("rbc", [128, T], F32)
        uhB = sb("uhB", [128, KD, HC], F32)
        uhC = sb("uhC", [128, KD, HC], F32)
        phB = sb("phB", [128, KD, HP], F32)
        phC = sb("phC", [128, KD, HP], F32)
        vecs = sb("vecs", [128, cfg.NV], F32)
        bsc = sb("bsc", [128, KD], F32)
        pT = sb("pT", [128, KP, T], BF16)
        ones = sb("ones", [128, 128], F32)
        ps = es.enter_context(nc.psum_tensor("ps", [128, 4, 2, 512], F32))

        sem_names = ["pe", "act", "dve", "x", "pt", "vec", "hist", "invc", "spill", "outy", "outs"] + \
                    ["ring%d" % i for i in range(NST)]
        sems = {n: es.enter_context(nc.semaphore(n)) for n in sem_names}

        xb = [Buf("x%d" % c) for c in range(KD)]
        hbb = [Buf("hb%d" % c) for c in range(KD)]
        psb = [Buf("ps%d" % i) for i in range(4)]
        ringb = [Buf("ring%d" % i) for i in range(NST)]
        tmpb = [Buf("tmp%d" % i) for i in range(NTMP)]
        rbcb = Buf("rbc")
        vecb = Buf("vecs")
        bscb = Buf("bsc")
        onesb = Buf("ones")
        pTb = Buf("pT")
        uhBb = [Buf() for _ in range(KD)]
        uhCb = [Buf() for _ in range(KD)]
        phBb = [Buf() for _ in range(KD)]
        phCb = [Buf() for _ in range(KD)]
        spillb = Buf("spill")
        r2b = {}

        def tmp_ap(i):
            return scr[:, i * T:(i + 1) * T]

        R2o = NTMP * T
        st = {"slot": 0, "tmp": 0, "k": 0}

        def next_slot():
            s = st["slot"]
            st["slot"] = (s + 1) % 4
            return s

        def next_tmp():
            i = st["tmp"]
            st["tmp"] = (i + 1) % NTMP
            return i

        def halves(ap_flat):
            return ap_flat.rearrange("p (a h) -> p a h", a=2)

        def psv(slot):
            return ps[:, slot, :, 0:H]

        def vcol(name, c, l=None):
            o = cfg.voff[(name, l)] if l is not None else cfg.voff[name]
            return vecs[:, o + c:o + c + 1]

        def stage(kind, ncols):
            k = st["k"]
            st["k"] = k + 1
            kk = k % NS
            assert plan[kk][0] == kind, (plan[kk], kind)
            s = k % NST
            S.op("pool", lambda e, s=s, kk=kk, ncols=ncols: e.dma_start(out=ring[:, s, 0:ncols], in_=w_all[kk, :, 0:ncols]),
                 writes=[ringb[s]], sem="ring%d" % s, inc=16)
            return s

        def mm_group(pairs, reads):
            slot = next_slot()

            def fn(e, pairs=pairs, slot=slot):
                n = len(pairs)
                ins = None
                for i, (lhsT, rhs) in enumerate(pairs):
                    for hh in range(2):
                        ins = e.matmul(ps[:, slot, hh, 0:H], lhsT=lhsT, rhs=rhs(hh),
                                       start=(i == 0), stop=(i == n - 1))
                return ins
            S.op("pe", fn, reads=reads, writes=[psb[slot]])
            return slot

        S.op("sp", lambda e: e.dma_start(out=vecs[:, :], in_=vecs_d[:, :]), writes=[vecb], sem="vec", inc=16)
        S.op("sp", lambda e: e.dma_start(out=phC[:, :, :], in_=spool.rearrange("(c q) j -> q c j", q=128)),
             writes=phCb, sem="hist", inc=16)
        S.op("sp", lambda e: e.dma_start(out=uhC[:, :, :], in_=sconv.rearrange("(c q) j -> q c j", q=128)),
             writes=uhCb, sem="hist", inc=16)
        for b in phCb + uhCb:
            b.w = ("hist", 32)
        S.op("dve", lambda e: e.memset(ones[:, :], 1.0), writes=[onesb])
        po, pso = cfg.voff["pool_b"], cfg.voff["pool_scale"]
        S.op("dve", lambda e: e.tensor_tensor(out=bsc[:, :], in0=vecs[:, po:po + KD], in1=vecs[:, pso:pso + KD], op=ALU.mult),
             reads=[vecb], writes=[bscb])

        def rms_stats(gname=None, l=None):
            slot = next_slot()
            for c in range(KD):
                if gname is not None:
                    S.op("act", lambda e, c=c: e.activation(out=hb[:, c, :], in_=xT[:, c, :], func=AF.Identity,
                                                            scale=vcol(gname, c, l)),
                         reads=[xb[c], vecb], writes=[hbb[c]])
                ti = next_tmp()
                S.op("act", lambda e, c=c, ti=ti: e.activation(out=tmp_ap(ti), in_=xT[:, c, :], func=AF.Square),
                     reads=[xb[c]], writes=[tmpb[ti]])

                def fn(e, c=c, ti=ti, slot=slot):
                    ins = None
                    for hh in range(2):
                        ins = e.matmul(ps[:, slot, hh, 0:H], lhsT=ones[:, :], rhs=tmp_ap(ti)[:, hh * H:(hh + 1) * H],
                                       start=(c == 0), stop=(c == KD - 1))
                    return ins
                S.op("pe", fn, reads=[tmpb[ti], onesb], writes=[psb[slot]])
            S.op("act", lambda e, slot=slot: e.activation(out=halves(rbc[:, :]), in_=psv(slot), func=AF.Sqrt,
                                                          bias=EPS, scale=1.0 / D),
                 reads=[psb[slot]], writes=[rbcb])
            S.op("dve", lambda e: e.reciprocal(out=rbc[:, :], in_=rbc[:, :]), reads=[rbcb], writes=[rbcb])

        def rms_to_hb(gname, l):
            rms_stats(gname, l)

        def ffn(l, which):
            act = scr[:, R2o:R2o + G * T].bitcast(BF16).rearrange("p (a g t) -> p a g t", a=2, g=G)
            actb = [[Buf() for _ in range(G)] for _ in range(2)]
            gi = 0
            for g0 in range(0, KF, G):
                grp = list(range(g0, min(g0 + G, KF)))
                par = gi % 2
                gi += 1
                for jj, j in enumerate(grp):
                    sg_ = stage("g", KD * 128)
                    slotg = mm_group([(ring[:, sg_, kc * 128:(kc + 1) * 128], (lambda hh, kc=kc: hb[:, kc, hh * H:(hh + 1) * H])) for kc in range(KD)],
                                     reads=[ringb[sg_]] + hbb)
                    su_ = stage("u", KD * 128)
                    slotu = mm_group([(ring[:, su_, kc * 128:(kc + 1) * 128], (lambda hh, kc=kc: hb[:, kc, hh * H:(hh + 1) * H])) for kc in range(KD)],
                                     reads=[ringb[su_]] + hbb)
                    ta, tb_ = next_tmp(), next_tmp()
                    S.op("dve", lambda e, ta=ta, slotg=slotg: e.tensor_tensor(
                        out=halves(tmp_ap(ta)), in0=psv(slotg), in1=halves(rbc[:, :]), op=ALU.mult),
                        reads=[psb[slotg], rbcb], writes=[tmpb[ta]])
                    S.op("act", lambda e, ta=ta: e.activation(out=tmp_ap(ta), in_=tmp_ap(ta), func=AF.Silu),
                         reads=[tmpb[ta]], writes=[tmpb[ta]])
                    S.op("dve", lambda e, tb_=tb_, slotu=slotu: e.tensor_tensor(
                        out=halves(tmp_ap(tb_)), in0=psv(slotu), in1=halves(rbc[:, :]), op=ALU.mult),
                        reads=[psb[slotu], rbcb], writes=[tmpb[tb_]])
                    S.op("dve", lambda e, ta=ta, tb_=tb_, par=par, jj=jj: e.tensor_tensor(
                        out=act[:, par, jj, :], in0=tmp_ap(tb_), in1=tmp_ap(ta), op=ALU.mult),
                        reads=[tmpb[ta], tmpb[tb_]], writes=[actb[par][jj]])
                ds = [stage("d", D) for _ in grp]
                for c in range(KD):
                    slot = mm_group([(ring[:, ds[jj], c * 128:(c + 1) * 128], (lambda hh, par=par, jj=jj: act[:, par, jj, hh * H:(hh + 1) * H])) for jj in range(len(grp))],
                                    reads=[ringb[s_] for s_ in ds] + actb[par][:len(grp)])
                    S.op("dve", lambda e, c=c, slot=slot: e.scalar_tensor_tensor(
                        out=halves(xT[:, c, :]), in0=psv(slot), scalar=0.5, in1=halves(xT[:, c, :]), op0=ALU.mult, op1=ALU.add),
                        reads=[psb[slot], xb[c]], writes=[xb[c]])

        def pool_mixer(p):
            segs = cfg.segs[p]
            S.set_fence()
            invt = scr[:, R2o:R2o + 4 * T].rearrange("p (g t) -> p g t", g=4)
            invb = Buf()
            S.op("sp", lambda e: e.dma_start(out=scr[:, R2o:R2o + 4 * T], in_=invc[p, :, :]), writes=[invb], sem="invc", inc=16,
                 deps=[(k, v) for k, v in S.fence.items()])
            eo = R2o + 4 * T
            Et = [scr[:, eo + i * NEP:eo + (i + 1) * NEP] for i in range(4)]
            Eb = [Buf() for _ in range(4)]
            regs = []
            ro = 0
            for (s0, n) in segs:
                regs.append((ro, s0, n))
                ro += HP + n
            NE = ro
            rms_stats()
            if p == 0:
                for i in range(2):
                    S.op("dve", lambda e, i=i: e.memset(Et[i][:, 0:HP], 0.0), writes=[Eb[i]])
            for c in range(KD):
                g = c // KG
                nsteps = g + 1
                ei = c % 2
                E = Et[ei]
                if p == 1:
                    S.op("act", lambda e, c=c, E=E: e.copy(out=E[:, regs[0][0]:regs[0][0] + HP], in_=phB[:, c, :]),
                         reads=[phBb[c]], writes=[Eb[ei]])
                    S.op("act", lambda e, c=c, E=E: e.copy(out=E[:, regs[1][0]:regs[1][0] + HP], in_=phC[:, c, :]),
                         reads=[phCb[c]], writes=[Eb[ei]])
                for (ro_, s0, n) in regs:
                    S.op("dve", lambda e, c=c, E=E, ro_=ro_, s0=s0, n=n: e.scalar_tensor_tensor(
                        out=E[:, ro_ + HP:ro_ + HP + n], in0=xT[:, c, s0:s0 + n], scalar=vcol("g_mix", c, 0),
                        in1=rbc[:, s0:s0 + n], op0=ALU.mult, op1=ALU.mult),
                        reads=[xb[c], rbcb, vecb, Eb[ei]], writes=[Eb[ei]])
                if p == 0:
                    a = regs[0][0] + HP + cfg.real_end0 - HP
                    S.op("act", lambda e, c=c, E=E, a=a: e.copy(out=phB[:, c, :], in_=E[:, a:a + HP]),
                         reads=[Eb[ei]], writes=[phBb[c]])
                else:
                    a = regs[0][0] + HP + regs[0][2] - HP
                    S.op("act", lambda e, c=c, E=E, a=a: e.copy(out=phB[:, c, :], in_=E[:, a:a + HP]),
                         reads=[Eb[ei]], writes=[phBb[c]])
                    a2 = regs[1][0] + HP + regs[1][2] - HP
                    S.op("act", lambda e, c=c, E=E, a2=a2: e.copy(out=phC[:, c, :], in_=E[:, a2:a2 + HP]),
                         reads=[Eb[ei]], writes=[phCb[c]])
                src, srcb = E, Eb[ei]
                for sidx in range(nsteps):
                    sh = 1 << sidx
                    lo = 2 * sh - 1
                    di = 2 + (sidx % 2)
                    dst, dstb = Et[di], Eb[di]
                    S.op("dve", lambda e, src=src, dst=dst, sh=sh, lo=lo: e.tensor_tensor(
                        out=dst[:, lo:NE], in0=src[:, lo:NE], in1=src[:, lo - sh:NE - sh], op=ALU.add),
                        reads=[srcb], writes=[dstb])
                    src, srcb = dst, dstb
                oi = 2 + (nsteps % 2)
                M, Mb = Et[oi], Eb[oi]
                for (ro_, s0, n) in regs:
                    a = ro_ + HP
                    S.op("dve", lambda e, src=src, M=M, a=a, n=n, s0=s0, g=g: e.tensor_tensor(
                        out=M[:, a:a + n], in0=src[:, a:a + n], in1=invt[:, g, s0:s0 + n], op=ALU.mult),
                        reads=[srcb, invb], writes=[Mb])
                    S.op("dve", lambda e, M=M, E=E, a=a, n=n, s0=s0, c=c: e.tensor_tensor(
                        out=hb[:, c, s0:s0 + n], in0=M[:, a:a + n], in1=E[:, a:a + n], op=ALU.subtract),
                        reads=[Mb, Eb[ei]], writes=[hbb[c]])
            if p == 1:
                S.op("sp", lambda e: e.dma_start(out=pool_out[0].rearrange("(c q) j -> q c j", q=128), in_=phB[:, :, :]),
                     reads=phBb, sem="outs", inc=16)
                S.op("sp", lambda e: e.dma_start(out=pool_out[1].rearrange("(c q) j -> q c j", q=128), in_=phC[:, :, :]),
                     reads=phCb, sem="outs", inc=16)
            for g in range(4):
                for j in range(KG):
                    oc = g * KG + j
                    s_ = stage("pool", KG * 128)
                    slot = mm_group([(ring[:, s_, kc * 128:(kc + 1) * 128], (lambda hh, g=g, kc=kc: hb[:, g * KG + kc, hh * H:(hh + 1) * H])) for kc in range(KG)],
                                    reads=[ringb[s_]] + hbb[g * KG:(g + 1) * KG])
                    ti = next_tmp()
                    S.op("act", lambda e, ti=ti, slot=slot, oc=oc: e.activation(
                        out=halves(tmp_ap(ti)), in_=psv(slot), func=AF.Identity, bias=bsc[:, oc:oc + 1],
                        scale=vcol("pool_scale", oc)),
                        reads=[psb[slot], bscb, vecb], writes=[tmpb[ti]])
                    S.op("dve", lambda e, ti=ti, oc=oc: e.tensor_tensor(out=xT[:, oc, :], in0=xT[:, oc, :], in1=tmp_ap(ti), op=ALU.add),
                         reads=[tmpb[ti], xb[oc]], writes=[xb[oc]])
            S.set_fence()

        def conv_mixer(p):
            segs = cfg.segs[p]
            S.set_fence()
            uo = R2o
            Ut = [scr[:, uo + i * NEC:uo + (i + 1) * NEC] for i in range(2)]
            Ub = [Buf() for _ in range(2)]
            bo = uo + 2 * NEC
            Ubf_all = scr[:, bo:bo + NEC].bitcast(BF16)
            Ubft = [Ubf_all[:, i * NEC:(i + 1) * NEC] for i in range(2)]
            Ubfb = [Buf() for _ in range(2)]
            mo = bo + NEC
            mean_t = scr[:, mo:mo + T]
            rstd_t = scr[:, mo + T:mo + 2 * T]
            meanb, rstdb = Buf(), Buf()
            regs = []
            ro = 0
            for (s0, n) in segs:
                regs.append((ro, s0, n))
                ro += HC + n
            NE = ro
            NEO = NE - HC
            H2 = NEO // 2
            assert NEO % 2 == 0 and H2 <= 512
            pieces = []
            for (ro_, s0, n) in regs:
                for hh in range(2):
                    lo_, hi_ = max(ro_, hh * H2), min(ro_ + n, (hh + 1) * H2)
                    if hi_ > lo_:
                        pieces.append((hh, lo_ - hh * H2, s0 + lo_ - ro_, hi_ - lo_))
            rms_to_hb("g_mix", 1)
            S.op("sp", lambda e: e.dma_start(out=xspill.rearrange("(c q) t -> q c t", q=128), in_=xT[:, :, :]),
                 reads=xb, writes=[spillb], sem="spill", inc=16)
            if p == 0:
                for i in range(2):
                    S.op("dve", lambda e, i=i: e.memset(Ut[i][:, 0:HC], 0.0), writes=[Ub[i]])

            def dwconv(c):
                ui = c % 2
                s_ = stage("dw", CW * 128)
                slot = next_slot()

                def fn(e, s_=s_, slot=slot, ui=ui):
                    ins = None
                    for hh in range(2):
                        for k in range(CW):
                            ins = e.matmul(ps[:, slot, hh, 0:H2], lhsT=ring[:, s_, k * 128:(k + 1) * 128],
                                           rhs=Ubft[ui][:, hh * H2 + k:hh * H2 + k + H2], start=(k == 0), stop=(k == CW - 1))
                    return ins
                S.op("pe", fn, reads=[ringb[s_], Ubfb[ui]], writes=[psb[slot]])
                for (hh, a_, xc, n) in pieces:
                    S.op("act", lambda e, c=c, slot=slot, hh=hh, a_=a_, xc=xc, n=n: e.activation(
                        out=xT[:, c, xc:xc + n], in_=ps[:, slot, hh, a_:a_ + n], func=AF.Identity, bias=vcol("b_dw", c)),
                        reads=[psb[slot], vecb], writes=[xb[c]])

            for c in range(KD):
                sv_ = stage("pw1", KD * 128)
                slotv = mm_group([(ring[:, sv_, kc * 128:(kc + 1) * 128], (lambda hh, kc=kc: hb[:, kc, hh * H:(hh + 1) * H])) for kc in range(KD)],
                                 reads=[ringb[sv_]] + hbb)
                sg_ = stage("pw1", KD * 128)
                slotg = mm_group([(ring[:, sg_, kc * 128:(kc + 1) * 128], (lambda hh, kc=kc: hb[:, kc, hh * H:(hh + 1) * H])) for kc in range(KD)],
                                 reads=[ringb[sg_]] + hbb)
                tv, tg = next_tmp(), next_tmp()
                S.op("dve", lambda e, tg=tg, slotg=slotg: e.tensor_tensor(
                    out=halves(tmp_ap(tg)), in0=psv(slotg), in1=halves(rbc[:, :]), op=ALU.mult),
                    reads=[psb[slotg], rbcb], writes=[tmpb[tg]])
                S.op("act", lambda e, tg=tg, c=c: e.activation(
                    out=tmp_ap(tg), in_=tmp_ap(tg), func=AF.Sigmoid, bias=vcol("b_pw1", KD + c)),
                    reads=[tmpb[tg], vecb], writes=[tmpb[tg]])
                S.op("dve", lambda e, tv=tv, slotv=slotv: e.tensor_tensor(
                    out=halves(tmp_ap(tv)), in0=psv(slotv), in1=halves(rbc[:, :]), op=ALU.mult),
                    reads=[psb[slotv], rbcb], writes=[tmpb[tv]])
                ui = c % 2
                U = Ut[ui]
                if p == 1:
                    S.op("act", lambda e, c=c, U=U: e.copy(out=U[:, regs[0][0]:regs[0][0] + HC], in_=uhB[:, c, :]),
                         reads=[uhBb[c]], writes=[Ub[ui]])
                    S.op("act", lambda e, c=c, U=U: e.copy(out=U[:, regs[1][0]:regs[1][0] + HC], in_=uhC[:, c, :]),
                         reads=[uhCb[c]], writes=[Ub[ui]])
                for (ro_, s0, n) in regs:
                    S.op("dve", lambda e, U=U, ro_=ro_, s0=s0, n=n, tv=tv, tg=tg, c=c: e.scalar_tensor_tensor(
                        out=U[:, ro_ + HC:ro_ + HC + n], in0=tmp_ap(tv)[:, s0:s0 + n], scalar=vcol("b_pw1", c),
                        in1=tmp_ap(tg)[:, s0:s0 + n], op0=ALU.add, op1=ALU.mult),
                        reads=[tmpb[tv], tmpb[tg], Ub[ui], vecb], writes=[Ub[ui]])
                if p == 0:
                    mo_ = cfg.voff["mask"]
                    S.op("dve", lambda e, U=U: e.tensor_scalar(out=U[:, HC:HC + HALO], in0=U[:, HC:HC + HALO],
                                                              scalar1=vecs[:, mo_:mo_ + 1], scalar2=None, op0=ALU.mult),
                         reads=[Ub[ui], vecb], writes=[Ub[ui]])
                    a = HC + cfg.real_end0 - HC
                    S.op("act", lambda e, c=c, U=U, a=a: e.copy(out=uhB[:, c, :], in_=U[:, a:a + HC]),
                         reads=[Ub[ui]], writes=[uhBb[c]])
                else:
                    a = regs[0][0] + HC + regs[0][2] - HC
                    S.op("act", lambda e, c=c, U=U, a=a: e.copy(out=uhB[:, c, :], in_=U[:, a:a + HC]),
                         reads=[Ub[ui]], writes=[uhBb[c]])
                    a2 = regs[1][0] + HC + regs[1][2] - HC
                    S.op("act", lambda e, c=c, U=U, a2=a2: e.copy(out=uhC[:, c, :], in_=U[:, a2:a2 + HC]),
                         reads=[Ub[ui]], writes=[uhCb[c]])
                S.op("act", lambda e, U=U, ui=ui: e.copy(out=Ubft[ui][:, 0:NE], in_=U[:, 0:NE]),
                     reads=[Ub[ui]], writes=[Ubfb[ui]])
                if c >= 1:
                    dwconv(c - 1)
            dwconv(KD - 1)
            if p == 1:
                S.op("sp", lambda e: e.dma_start(out=conv_out[0].rearrange("(c q) j -> q c j", q=128), in_=uhB[:, :, :]),
                     reads=uhBb, sem="outs", inc=16)
                S.op("sp", lambda e: e.dma_start(out=conv_out[1].rearrange("(c q) j -> q c j", q=128), in_=uhC[:, :, :]),
                     reads=uhCb, sem="outs", inc=16)
            s1, s2 = next_slot(), next_slot()
            for c in range(KD):
                ti = next_tmp()
                S.op("act", lambda e, c=c, ti=ti: e.activation(out=tmp_ap(ti), in_=xT[:, c, :], func=AF.Square),
                     reads=[xb[c]], writes=[tmpb[ti]])

                def fn(e, c=c, ti=ti):
                    ins = None
                    for hh in range(2):
                        e.matmul(ps[:, s1, hh, 0:H], lhsT=ones[:, :], rhs=xT[:, c, hh * H:(hh + 1) * H],
                                 start=(c == 0), stop=(c == KD - 1))
                        ins = e.matmul(ps[:, s2, hh, 0:H], lhsT=ones[:, :], rhs=tmp_ap(ti)[:, hh * H:(hh + 1) * H],
                                       start=(c == 0), stop=(c == KD - 1))
                    return ins
                S.op("pe", fn, reads=[xb[c], tmpb[ti], onesb], writes=[psb[s1], psb[s2]])
            S.op("act", lambda e: e.activation(out=halves(mean_t), in_=psv(s1), func=AF.Identity, scale=1.0 / D),
                 reads=[psb[s1]], writes=[meanb])
            tq = next_tmp()
            S.op("dve", lambda e: e.tensor_tensor(out=tmp_ap(tq), in0=mean_t, in1=mean_t, op=ALU.mult),
                 reads=[meanb], writes=[tmpb[tq]])
            S.op("dve", lambda e: e.scalar_tensor_tensor(out=halves(rstd_t), in0=psv(s2), scalar=1.0 / D, in1=halves(tmp_ap(tq)),
                                                         op0=ALU.mult, op1=ALU.subtract),
                 reads=[psb[s2], tmpb[tq]], writes=[rstdb])
            S.op("act", lambda e: e.activation(out=rstd_t, in_=rstd_t, func=AF.Sqrt, bias=EPS),
                 reads=[rstdb], writes=[rstdb])
            S.op("dve", lambda e: e.reciprocal(out=rstd_t, in_=rstd_t), reads=[rstdb], writes=[rstdb])
            for c in range(KD):
                t1, t2 = next_tmp(), next_tmp()
                S.op("dve", lambda e, c=c, t1=t1: e.tensor_tensor(out=tmp_ap(t1), in0=xT[:, c, :], in1=mean_t, op=ALU.subtract),
                     reads=[xb[c], meanb], writes=[tmpb[t1]])
                S.op("dve", lambda e, t1=t1, t2=t2: e.tensor_tensor(out=tmp_ap(t2), in0=tmp_ap(t1), in1=rstd_t, op=ALU.mult),
                     reads=[tmpb[t1], rstdb], writes=[tmpb[t2]])
                S.op("act", lambda e, c=c, t2=t2: e.activation(out=hb[:, c, :], in_=tmp_ap(t2), func=AF.Silu,
                                                              bias=vcol("ln_b", c), scale=vcol("ln_g", c)),
                     reads=[tmpb[t2], vecb], writes=[hbb[c]])
            S.op("sp", lambda e: e.dma_start(out=xT[:, :, :], in_=xspill.rearrange("(c q) t -> q c t", q=128)),
                 reads=[spillb], writes=xb, sem="spill", inc=16)
            for oc in range(KD):
                s_ = stage("pw2", KD * 128)
                slot = mm_group([(ring[:, s_, kc * 128:(kc + 1) * 128], (lambda hh, kc=kc: hb[:, kc, hh * H:(hh + 1) * H])) for kc in range(KD)],
                                reads=[ringb[s_]] + hbb)
                S.op("dve", lambda e, oc=oc, slot=slot: e.scalar_tensor_tensor(
                    out=halves(xT[:, oc, :]), in0=psv(slot), scalar=vcol("b_pw2", oc), in1=halves(xT[:, oc, :]),
                    op0=ALU.add, op1=ALU.add),
                    reads=[psb[slot], xb[oc], vecb], writes=[xb[oc]])
            S.set_fence()

        def ple(p, l):
            S.op("pool", lambda e: e.dma_start(out=pT[:, :, :], in_=pin[p * 2 + l].rearrange("(k q) t -> q k t", q=128)),
                 writes=[pTb], sem="pt", inc=16)
            rms_to_hb("g_ple", l)
            for c in range(KD):
                s_ = stage("ple", SW)
                slotg = mm_group([(ring[:, s_, kc * 128:(kc + 1) * 128], (lambda hh, kc=kc: hb[:, kc, hh * H:(hh + 1) * H])) for kc in range(KD)],
                                 reads=[ringb[s_]] + hbb)
                slotp = mm_group([(ring[:, s_, (KD + kc) * 128:(KD + kc + 1) * 128], (lambda hh, kc=kc: pT[:, kc, hh * H:(hh + 1) * H])) for kc in range(KP)],
                                 reads=[ringb[s_], pTb])
                tg, tm = next_tmp(), next_tmp()
                S.op("dve", lambda e, tg=tg, slotg=slotg: e.tensor_tensor(
                    out=halves(tmp_ap(tg)), in0=psv(slotg), in1=halves(rbc[:, :]), op=ALU.mult),
                    reads=[psb[slotg], rbcb], writes=[tmpb[tg]])
                S.op("act", lambda e, tg=tg: e.activation(out=tmp_ap(tg), in_=tmp_ap(tg), func=AF.Sigmoid),
                     reads=[tmpb[tg]], writes=[tmpb[tg]])
                S.op("dve", lambda e, tg=tg, tm=tm, slotp=slotp: e.tensor_tensor(
                    out=halves(tmp_ap(tm)), in0=psv(slotp), in1=halves(tmp_ap(tg)), op=ALU.mult),
                    reads=[psb[slotp], tmpb[tg]], writes=[tmpb[tm]])
                S.op("dve", lambda e, c=c, tm=tm: e.tensor_tensor(out=xT[:, c, :], in0=xT[:, c, :], in1=tmp_ap(tm), op=ALU.add),
                     reads=[tmpb[tm], xb[c]], writes=[xb[c]])

        for p in range(2):
            S.op("sp", lambda e, p=p: e.dma_start(out=xT[:, :, :], in_=xin[p].rearrange("(c q) t -> q c t", q=128)),
                 writes=xb, sem="x", inc=16)
            for l in range(2):
                rms_to_hb("g_ffn1", l)
                ffn(l, 1)
                if l == 0:
                    pool_mixer(p)
                else:
                    conv_mixer(p)
                rms_to_hb("g_ffn2", l)
                ffn(l, 2)
                ple(p, l)
            rms_stats()
            for c in range(KD):
                S.op("dve", lambda e, c=c: e.scalar_tensor_tensor(out=xT[:, c, :], in0=xT[:, c, :], scalar=vcol("g_final", c),
                                                                  in1=rbc[:, :], op0=ALU.mult, op1=ALU.mult),
                     reads=[xb[c], rbcb, vecb], writes=[xb[c]])
            S.op("sp", lambda e, p=p: e.dma_start(out=yout[p].rearrange("(c q) t -> q c t", q=128), in_=xT[:, :, :]),
                 reads=xb, sem="outy", inc=16)
        assert st["k"] == 2 * NS
        final_waits = [(k, S.count[k]) for k in ("outy", "outs", "spill") if k in S.count]

        with nc.Block() as block:
            @block.tensor
            def _(e):
                S.emit("pe", e, sems)

            @block.scalar
            def _(e):
                S.emit("act", e, sems)

            @block.vector
            def _(e):
                S.emit("dve", e, sems)

            @block.gpsimd
            def _(e):
                S.emit("pool", e, sems)

            @block.sync
            def _(e):
                S.emit("sp", e, sems)
                for k, v in final_waits:
                    e.wait_ge(sems[k], v)
    return nc


def _tile_w(W, KD):
    K, N = W.shape
    return np.ascontiguousarray(W.reshape(KD, 128, N // 128, 128).transpose(2, 1, 0, 3)).reshape(N // 128, 128, KD * 128)


def _vec_cols(v):
    v = np.asarray(v, np.float32).reshape(-1)
    return np.ascontiguousarray(v.reshape(-1, 128).T)


def build_w_all(cfg, inp):
    plan = stage_plan(cfg)
    NS = len(plan)
    KD, KF, KP, KG = cfg.KD, cfg.KF, cfg.KP, cfg.KG
    W = np.zeros((NS, 128, cfg.SW), np.float32)
    idx = {}
    for k, e in enumerate(plan):
        if e[0] in ("g", "u", "d"):
            key = (e[0], e[1], e[2])
        elif e[0] == "ple":
            key = ("ple", e[1])
        else:
            key = (e[0],)
        idx.setdefault(key, []).append(k)
    for l in range(2):
        for which in (1, 2):
            names = {1: ("ffn1_w_gate", "ffn1_w_up", "ffn1_w_down"), 2: ("ffn2_w_gate", "ffn2_w_up", "ffn2_w_down")}[which]
            wg, wu, wd = inp[names[0]][l], inp[names[1]][l], inp[names[2]][l]
            W[idx[("g", l, which)], :, :KD * 128] = _tile_w(wg, KD)
            W[idx[("u", l, which)], :, :KD * 128] = _tile_w(wu, KD)
            W[idx[("d", l, which)], :, :cfg.D] = wd.reshape(KF, 128, cfg.D)
    pk = idx[("pool",)]
    pw = inp["pool_w"][0]
    for g in range(4):
        W[pk[g * KG:(g + 1) * KG], :, :KG * 128] = _tile_w(pw[g], KG)
    t1 = _tile_w(inp["conv_w_pw1"][0], KD)
    order = []
    for c in range(KD):
        order += [c, KD + c]
    W[idx[("pw1",)], :, :KD * 128] = t1[order]
    W[idx[("pw2",)], :, :KD * 128] = _tile_w(inp["conv_w_pw2"][0], KD)
    wdw = inp["conv_w_dw"][0].reshape(CW, KD, 128)
    dwt = np.zeros((KD, 128, CW * 128), np.float32)
    qq = np.arange(128)
    for k in range(CW):
        dwt[:, qq, k * 128 + qq] = wdw[k]
    W[idx[("dw",)], :, :CW * 128] = dwt
    for l in range(2):
        kk = idx[("ple", l)]
        W[kk, :, :KD * 128] = _tile_w(inp["ple_w_gate"][l], KD)
        W[kk, :, KD * 128:(KD + KP) * 128] = _tile_w(inp["ple_w_proj"][l], KP)
    return W


def prep_inputs(cfg, inp):
    D, T, KD = cfg.D, cfg.T, cfg.KD
    inp = {k: np.asarray(v) for k, v in inp.items()}
    w_all = build_w_all(cfg, inp)
    vecs = np.zeros((128, cfg.NV), np.float32)
    for l in range(2):
        for nm in ("g_ffn1", "g_mix", "g_ffn2", "g_ple"):
            o = cfg.voff[(nm, l)]
            vecs[:, o:o + KD] = _vec_cols(inp[nm][l])
    for nm, src in (("g_final", inp["g_final"]), ("pool_b", inp["pool_b"][0]), ("pool_scale", inp["pool_scale"][0]),
                    ("b_pw1", inp["conv_b_pw1"][0]), ("b_dw", inp["conv_b_dw"][0]), ("ln_g", inp["conv_ln_g"][0]),
                    ("ln_b", inp["conv_ln_b"][0]), ("b_pw2", inp["conv_b_pw2"][0])):
        o = cfg.voff[nm]
        vc = _vec_cols(src)
        vecs[:, o:o + vc.shape[1]] = vc
    xp, xs = inp["x_prompt"], inp["x_sample"]
    pp, psm = inp["p_prompt"], inp["p_sample"]
    in_maps = []
    for c in range(N_CORES):
        b, q = c // cfg.CPB, c % cfg.CPB
        s = q * cfg.NP
        xin = np.zeros((2, T, D), np.float32)
        pin = np.zeros((2, 2, T, cfg.PLE), np.float32)
        lo = s - HALO
        src_lo = max(lo, 0)
        n_real = s + cfg.OWN1 - src_lo
        xin[0, src_lo - lo:src_lo - lo + n_real] = xp[b, src_lo:s + cfg.OWN1]
        pin[0, :, src_lo - lo:src_lo - lo + n_real] = pp[:, b, src_lo:s + cfg.OWN1]
        xin[1, :cfg.OWN2] = xp[b, s + cfg.OWN1:s + cfg.NP]
        pin[1, :, :cfg.OWN2] = pp[:, b, s + cfg.OWN1:s + cfg.NP]
        xin[1, cfg.OWN2:] = xs[c]
        pin[1, :, cfg.OWN2:] = psm[:, c]
        pos = np.zeros((2, T), np.int64)
        pos[0] = lo + np.arange(T)
        pos[1, :cfg.OWN2] = s + cfg.OWN1 + np.arange(cfg.OWN2)
        pos[1, cfg.OWN2:] = cfg.PAST_LEN + np.arange(cfg.DEC_SEQ)
        inv = np.zeros((2, 4, T), np.float32)
        for g, win in enumerate(POOL_WINDOWS):
            cnt = np.minimum(np.maximum(pos, 0) + 1, win).astype(np.float32)
            inv[:, g, :] = (np.float32(1.0) / cnt).astype(np.float32)
        invc = np.ascontiguousarray(np.broadcast_to(inv.reshape(2, 1, 4 * T), (2, 128, 4 * T)))
        vc = vecs.copy()
        vc[:, cfg.voff["mask"]] = 0.0 if q == 0 else 1.0
        in_maps.append({
            "xin": np.ascontiguousarray(xin.transpose(0, 2, 1)),
            "pin": np.ascontiguousarray(pin.transpose(0, 1, 3, 2)).reshape(4, cfg.PLE, T),
            "invc": invc,
            "spool": np.ascontiguousarray(inp["state_pool"][0, c].T),
            "sconv": np.ascontiguousarray(inp["state_conv"][0, c].T),
            "vecs": vc,
            "w_all": w_all,
        })
    return in_maps


def assemble(cfg, results):
    D = cfg.D
    y_prompt = np.zeros((cfg.BATCH, cfg.SEQ, D), np.float32)
    y_sample = np.zeros((cfg.DEC_BATCH, cfg.DEC_SEQ, D), np.float32)
    npp = np.zeros((1, cfg.BATCH, HP, D), np.float32)
    ncp = np.zeros((1, cfg.BATCH, HC, D), np.float32)
    nps = np.zeros((1, cfg.DEC_BATCH, HP, D), np.float32)
    ncs = np.zeros((1, cfg.DEC_BATCH, HC, D), np.float32)
    for c in range(N_CORES):
        r = results[c]
        b, q = c // cfg.CPB, c % cfg.CPB
        s = q * cfg.NP
        yo = r["yout"]
        y_prompt[b, s:s + cfg.OWN1] = yo[0][:, HALO:HALO + cfg.OWN1].T
        y_prompt[b, s + cfg.OWN1:s + cfg.NP] = yo[1][:, :cfg.OWN2].T
        y_sample[c] = yo[1][:, cfg.OWN2:].T
        nps[0, c] = r["pool_out"][1].T
        ncs[0, c] = r["conv_out"][1].T
        if q == cfg.CPB - 1:
            npp[0, b] = r["pool_out"][0].T
            ncp[0, b] = r["conv_out"][0].T
    return (y_prompt, y_sample, npp, ncp, nps, ncs)


def run(cfg, inputs, trace=False):
    nc = build_program(cfg)
    in_maps = prep_inputs(cfg, inputs)
    res = run_bass_kernel_spmd(nc, in_maps, core_ids=list(range(N_CORES)), trace=trace)
    return assemble(cfg, res.results), res


def kernel(**inputs):
    cfg = Cfg()
    outs, _ = run(cfg, inputs)
    return outs
```

```python
import numpy as np
import contextlib
import concourse.bass as bass
import concourse.mybir as mybir
from concourse.bass_utils import run_bass_kernel_spmd

F32 = mybir.dt.float32
BF16 = mybir.dt.bfloat16
AF = mybir.ActivationFunctionType
ALU = mybir.AluOpType

N_CORES = 8
POOL_WINDOWS = (2, 4, 8, 16)
HP = 15
HC = 30
CW = 31
HALO = 45


class Cfg:
    def __init__(self, D=4096, F=11008, PLE=256, SEQ=4096, BATCH=2, DEC_BATCH=8, DEC_SEQ=16,
                 PAST_LEN=1024, T=544, OWN1=496, NST=7, G=4):
        self.D, self.F, self.PLE, self.SEQ, self.BATCH = D, F, PLE, SEQ, BATCH
        self.DEC_BATCH, self.DEC_SEQ, self.PAST_LEN = DEC_BATCH, DEC_SEQ, PAST_LEN
        self.T, self.OWN1, self.NST, self.G = T, OWN1, NST, G
        self.KD, self.KF, self.KP = D // 128, F // 128, PLE // 128
        assert D % 512 == 0 and F % 128 == 0 and PLE % 128 == 0
        self.KG = self.KD // 4
        self.CPB = N_CORES // BATCH
        self.NP = SEQ // self.CPB
        self.OWN2 = T - DEC_SEQ
        assert self.OWN1 + self.OWN2 == self.NP, (self.OWN1, self.OWN2, self.NP)
        self.PAD1 = T - HALO - OWN1
        assert self.PAD1 >= 0 and T % 2 == 0
        assert DEC_BATCH == N_CORES
        self.H = T // 2
        self.SW = max((self.KD + self.KP) * 128, CW * 128)
        self.EPS = 1e-6
        KD = self.KD
        off = {}
        o = 0
        for l in range(2):
            for nm in ("g_ffn1", "g_mix", "g_ffn2", "g_ple"):
                off[(nm, l)] = o
                o += KD
        for nm, w in (("g_final", KD), ("pool_b", KD), ("pool_scale", KD), ("b_pw1", 2 * KD), ("b_dw", KD),
                      ("ln_g", KD), ("ln_b", KD), ("b_pw2", KD), ("mask", 1)):
            off[nm] = o
            o += w
        self.voff = off
        self.NV = o
        self.segs = [[(0, T)], [(0, self.OWN2), (self.OWN2, DEC_SEQ)]]
        self.real_end0 = HALO + OWN1


def stage_plan(cfg):
    plan = []
    KD, KF, G = cfg.KD, cfg.KF, cfg.G
    for l in range(2):
        def ffn(which):
            for g0 in range(0, KF, G):
                grp = list(range(g0, min(g0 + G, KF)))
                for j in grp:
                    plan.append(("g", l, which, j))
                    plan.append(("u", l, which, j))
                for j in grp:
                    plan.append(("d", l, which, j))
        ffn(1)
        if l == 0:
            for g in range(4):
                for j in range(cfg.KG):
                    plan.append(("pool", g, j))
        else:
            for c in range(KD):
                plan.append(("pw1", c))
                plan.append(("pw1", KD + c))
                if c >= 1:
                    plan.append(("dw", c - 1))
            plan.append(("dw", KD - 1))
            for c in range(KD):
                plan.append(("pw2", c))
        ffn(2)
        for c in range(KD):
            plan.append(("ple", l, c))
    return plan


class Buf:
    __slots__ = ("w", "r", "name")

    def __init__(self, name=""):
        self.w = None
        self.r = {}
        self.name = name


ENG_SEM = {"pe": "pe", "act": "act", "dve": "dve"}


class Sched:
    def __init__(self):
        self.ops = {e: [] for e in ("pe", "act", "dve", "pool", "sp")}
        self.count = {}
        self.waited = {e: {} for e in self.ops}
        self.fence = {}

    def set_fence(self):
        self.fence = {k: self.count.get(k, 0) for k in ("pe", "act", "dve")}

    def op(self, eng, fn, reads=(), writes=(), sem=None, inc=1, deps=()):
        need = {}

        def add(k, v):
            if v and need.get(k, 0) < v:
                need[k] = v

        for b in reads:
            if b.w is not None:
                add(*b.w)
        for b in writes:
            if b.w is not None:
                add(*b.w)
            for k, v in b.r.items():
                add(k, v)
        for t in deps:
            if t is not None:
                add(*t)
        if eng in ("pe", "act", "dve"):
            for k, v in self.fence.items():
                add(k, v)
        if eng == "pe":
            need.pop("pe", None)
        wd = self.waited[eng]
        waits = []
        for k, v in need.items():
            if wd.get(k, 0) < v:
                wd[k] = v
                waits.append((k, v))
        semk = sem or ENG_SEM[eng]
        self.count[semk] = self.count.get(semk, 0) + inc
        tok = (semk, self.count[semk])
        self.ops[eng].append((waits, fn, (semk, inc)))
        for b in reads:
            if b.r.get(semk, 0) < tok[1]:
                b.r[semk] = tok[1]
        for b in writes:
            b.w = tok
            b.r = {}
        return tok

    def emit(self, eng, e, sems):
        for waits, fn, (semk, inc) in self.ops[eng]:
            for k, v in waits:
                e.wait_ge(sems[k], v)
            ins = fn(e)
            ins.then_inc(sems[semk], inc)


def build_program(cfg):
    D, KD, KF, KP, KG, T, H, G, NST, SW = cfg.D, cfg.KD, cfg.KF, cfg.KP, cfg.KG, cfg.T, cfg.H, cfg.G, cfg.NST, cfg.SW
    EPS = cfg.EPS
    plan = stage_plan(cfg)
    NS = len(plan)
    nc = bass.Bass("TRN2", target_bir_lowering=False)

    def din(name, shape):
        return nc.dram_tensor(name, list(shape), F32, kind="ExternalInput").ap()

    def dout(name, shape):
        return nc.dram_tensor(name, list(shape), F32, kind="ExternalOutput").ap()

    xin = din("xin", [2, D, T])
    pin = din("pin", [2 * 2, KP * 128, T])
    invc = din("invc", [2, 128, 4 * T])
    spool = din("spool", [D, HP])
    sconv = din("sconv", [D, HC])
    vecs_d = din("vecs", [128, cfg.NV])
    w_all = din("w_all", [NS, 128, SW])
    yout = dout("yout", [2, D, T])
    pool_out = dout("pool_out", [2, D, HP])
    conv_out = dout("conv_out", [2, D, HC])
    xspill = dout("xspill", [D, T])

    NTMP = 4
    NEP = T + 2 * HP
    NEC = T + 2 * HC
    r2_ffn = G * T
    r2_pool = 4 * T + 3 * NEP
    r2_conv = 3 * NEC + 4 * T
    R2 = max(r2_ffn, r2_pool, r2_conv)
    SCR = NTMP * T + R2

    S = Sched()
    with contextlib.ExitStack() as es:
        def sb(name, shape, dt):
            return es.enter_context(nc.sbuf_tensor("sb_" + name, list(shape), dt))

        xT = sb("xT", [128, KD, T], F32)
        hb = sb("hb", [128, KD, T], BF16)
        ring = sb("ring", [128, NST, SW], BF16)
        scr = sb("scr", [128, SCR], F32)
        rbc = sb("rbc", [128, T], F32)
        sacc = sb("sacc", [128, T], F32)
        uhB = sb("uhB", [128, KD, HC], F32)
        uhC = sb("uhC", [128, KD, HC], F32)
        phB = sb("phB", [128, KD, HP], F32)
        phC = sb("phC", [128, KD, HP], F32)
        vecs = sb("vecs", [128, cfg.NV], F32)
        bsc = sb("bsc", [128, KD], F32)
        pT = sb("pT", [128, KP, T], BF16)
        ones = sb("ones", [128, 128], F32)
        ps = es.enter_context(nc.psum_tensor("ps", [128, 4, 2, 512], F32))

        sem_names = ["pe", "act", "dve", "x", "pt", "vec", "hist", "invc", "spill", "outy", "outs"] + \
                    ["ring%d" % i for i in range(NST)]
        sems = {n: es.enter_context(nc.semaphore(n)) for n in sem_names}

        xb = [Buf("x%d" % c) for c in range(KD)]
        hbb = [Buf("hb%d" % c) for c in range(KD)]
        psb = [Buf("ps%d" % i) for i in range(4)]
        ringb = [Buf("ring%d" % i) for i in range(NST)]
        tmpb = [Buf("tmp%d" % i) for i in range(NTMP)]
        rbcb = Buf("rbc")
        saccb = Buf("sacc")
        vecb = Buf("vecs")
        bscb = Buf("bsc")
        onesb = Buf("ones")
        pTb = Buf("pT")
        uhBb = [Buf() for _ in range(KD)]
        uhCb = [Buf() for _ in range(KD)]
        phBb = [Buf() for _ in range(KD)]
        phCb = [Buf() for _ in range(KD)]
        spillb = Buf("spill")
        r2b = {}

        def tmp_ap(i):
            return scr[:, i * T:(i + 1) * T]

        R2o = NTMP * T
        st = {"slot": 0, "tmp": 0, "k": 0}

        def next_slot():
            s = st["slot"]
            st["slot"] = (s + 1) % 4
            return s

        def next_tmp():
            i = st["tmp"]
            st["tmp"] = (i + 1) % NTMP
            return i

        def halves(ap_flat):
            return ap_flat.rearrange("p (a h) -> p a h", a=2)

        def psv(slot):
            return ps[:, slot, :, 0:H]

        def vcol(name, c, l=None):
            o = cfg.voff[(name, l)] if l is not None else cfg.voff[name]
            return vecs[:, o + c:o + c + 1]

        def stage(kind, ncols):
            k = st["k"]
            st["k"] = k + 1
            kk = k % NS
            assert plan[kk][0] == kind, (plan[kk], kind)
            s = k % NST
            S.op("pool", lambda e, s=s, kk=kk, ncols=ncols: e.dma_start(out=ring[:, s, 0:ncols], in_=w_all[kk, :, 0:ncols]),
                 writes=[ringb[s]], sem="ring%d" % s, inc=16)
            return s

        def mm_group(pairs, reads):
            slot = next_slot()

            def fn(e, pairs=pairs, slot=slot):
                n = len(pairs)
                ins = None
                for i, (lhsT, rhs) in enumerate(pairs):
                    for hh in range(2):
                        ins = e.matmul(ps[:, slot, hh, 0:H], lhsT=lhsT, rhs=rhs(hh),
                                       start=(i == 0), stop=(i == n - 1))
                return ins
            S.op("pe", fn, reads=reads, writes=[psb[slot]])
            if st.get("pending") is not None:
                f = st["pending"]
                st["pending"] = None
                f()
            return slot

        S.op("sp", lambda e: e.dma_start(out=vecs[:, :], in_=vecs_d[:, :]), writes=[vecb], sem="vec", inc=16)
        S.op("sp", lambda e: e.dma_start(out=phC[:, :, :], in_=spool.rearrange("(c q) j -> q c j", q=128)),
             writes=phCb, sem="hist", inc=16)
        S.op("sp", lambda e: e.dma_start(out=uhC[:, :, :], in_=sconv.rearrange("(c q) j -> q c j", q=128)),
             writes=uhCb, sem="hist", inc=16)
        for b in phCb + uhCb:
            b.w = ("hist", 32)
        S.op("dve", lambda e: e.memset(ones[:, :], 1.0), writes=[onesb])
        po, pso = cfg.voff["pool_b"], cfg.voff["pool_scale"]
        S.op("dve", lambda e: e.tensor_tensor(out=bsc[:, :], in0=vecs[:, po:po + KD], in1=vecs[:, pso:pso + KD], op=ALU.mult),
             reads=[vecb], writes=[bscb])

        def rms_stats(gname=None, l=None):
            for c in range(KD):
                if gname is not None:
                    S.op("act", lambda e, c=c: e.activation(out=hb[:, c, :], in_=xT[:, c, :], func=AF.Identity,
                                                            scale=vcol(gname, c, l)),
                         reads=[xb[c], vecb], writes=[hbb[c]])
                if c == 0:
                    S.op("act", lambda e: e.activation(out=sacc[:, :], in_=xT[:, 0, :], func=AF.Square),
                         reads=[xb[0]], writes=[saccb])
                else:
                    ti = next_tmp()
                    S.op("act", lambda e, c=c, ti=ti: e.activation(out=tmp_ap(ti), in_=xT[:, c, :], func=AF.Square),
                         reads=[xb[c]], writes=[tmpb[ti]])
                    S.op("dve", lambda e, ti=ti: e.tensor_tensor(out=sacc[:, :], in0=sacc[:, :], in1=tmp_ap(ti), op=ALU.add),
                         reads=[tmpb[ti], saccb], writes=[saccb])

            def finish():
                slot = next_slot()

                def fn(e, slot=slot):
                    ins = None
                    for hh in range(2):
                        ins = e.matmul(ps[:, slot, hh, 0:H], lhsT=ones[:, :], rhs=sacc[:, hh * H:(hh + 1) * H],
                                       start=True, stop=True)
                    return ins
                S.op("pe", fn, reads=[saccb, onesb], writes=[psb[slot]])
                S.op("act", lambda e, slot=slot: e.activation(out=halves(rbc[:, :]), in_=psv(slot), func=AF.Sqrt,
                                                              bias=EPS, scale=1.0 / D),
                     reads=[psb[slot]], writes=[rbcb])
                S.op("dve", lambda e: e.reciprocal(out=rbc[:, :], in_=rbc[:, :]), reads=[rbcb], writes=[rbcb])
            if gname is None:
                finish()
            else:
                st["pending"] = finish

        def rms_to_hb(gname, l):
            rms_stats(gname, l)

        def ffn(l, which):
            act = scr[:, R2o:R2o + G * T].bitcast(BF16).rearrange("p (a g t) -> p a g t", a=2, g=G)
            actb = [[Buf() for _ in range(G)] for _ in range(2)]
            gi = 0
            for g0 in range(0, KF, G):
                grp = list(range(g0, min(g0 + G, KF)))
                par = gi % 2
                gi += 1
                for jj, j in enumerate(grp):
                    sg_ = stage("g", KD * 128)
                    slotg = mm_group([(ring[:, sg_, kc * 128:(kc + 1) * 128], (lambda hh, kc=kc: hb[:, kc, hh * H:(hh + 1) * H])) for kc in range(KD)],
                                     reads=[ringb[sg_]] + hbb)
                    su_ = stage("u", KD * 128)
                    slotu = mm_group([(ring[:, su_, kc * 128:(kc + 1) * 128], (lambda hh, kc=kc: hb[:, kc, hh * H:(hh + 1) * H])) for kc in range(KD)],
                                     reads=[ringb[su_]] + hbb)
                    ta, tb_ = next_tmp(), next_tmp()
                    S.op("dve", lambda e, ta=ta, slotg=slotg: e.tensor_tensor(
                        out=halves(tmp_ap(ta)), in0=psv(slotg), in1=halves(rbc[:, :]), op=ALU.mult),
                        reads=[psb[slotg], rbcb], writes=[tmpb[ta]])
                    S.op("act", lambda e, ta=ta: e.activation(out=tmp_ap(ta), in_=tmp_ap(ta), func=AF.Silu),
                         reads=[tmpb[ta]], writes=[tmpb[ta]])
                    S.op("dve", lambda e, tb_=tb_, slotu=slotu: e.tensor_tensor(
                        out=halves(tmp_ap(tb_)), in0=psv(slotu), in1=halves(rbc[:, :]), op=ALU.mult),
                        reads=[psb[slotu], rbcb], writes=[tmpb[tb_]])
                    S.op("dve", lambda e, ta=ta, tb_=tb_, par=par, jj=jj: e.tensor_tensor(
                        out=act[:, par, jj, :], in0=tmp_ap(tb_), in1=tmp_ap(ta), op=ALU.mult),
                        reads=[tmpb[ta], tmpb[tb_]], writes=[actb[par][jj]])
                ds = [stage("d", D) for _ in grp]
                for c in range(KD):
                    slot = mm_group([(ring[:, ds[jj], c * 128:(c + 1) * 128], (lambda hh, par=par, jj=jj: act[:, par, jj, hh * H:(hh + 1) * H])) for jj in range(len(grp))],
                                    reads=[ringb[s_] for s_ in ds] + actb[par][:len(grp)])
                    S.op("dve", lambda e, c=c, slot=slot: e.scalar_tensor_tensor(
                        out=halves(xT[:, c, :]), in0=psv(slot), scalar=0.5, in1=halves(xT[:, c, :]), op0=ALU.mult, op1=ALU.add),
                        reads=[psb[slot], xb[c]], writes=[xb[c]])

        def pool_mixer(p):
            segs = cfg.segs[p]
            S.set_fence()
            invt = scr[:, R2o:R2o + 4 * T].rearrange("p (g t) -> p g t", g=4)
            invb = Buf()
            S.op("sp", lambda e: e.dma_start(out=scr[:, R2o:R2o + 4 * T], in_=invc[p, :, :]), writes=[invb], sem="invc", inc=16,
                 deps=[(k, v) for k, v in S.fence.items()])
            eo = R2o + 4 * T
            Et = [scr[:, eo + i * NEP:eo + (i + 1) * NEP] for i in (0, 0, 1, 2)]
            Eb = [Buf() for _ in range(4)]
            Eb[1] = Eb[0]
            regs = []
            ro = 0
            for (s0, n) in segs:
                regs.append((ro, s0, n))
                ro += HP + n
            NE = ro
            rms_stats()
            if p == 0:
                S.op("dve", lambda e: e.memset(Et[0][:, 0:HP], 0.0), writes=[Eb[0]])
            for c in range(KD):
                g = c // KG
                nsteps = g + 1
                ei = c % 2
                E = Et[ei]
                if p == 1:
                    S.op("act", lambda e, c=c, E=E: e.copy(out=E[:, regs[0][0]:regs[0][0] + HP], in_=phB[:, c, :]),
                         reads=[phBb[c]], writes=[Eb[ei]])
                    S.op("act", lambda e, c=c, E=E: e.copy(out=E[:, regs[1][0]:regs[1][0] + HP], in_=phC[:, c, :]),
                         reads=[phCb[c]], writes=[Eb[ei]])
                for (ro_, s0, n) in regs:
                    S.op("dve", lambda e, c=c, E=E, ro_=ro_, s0=s0, n=n: e.scalar_tensor_tensor(
                        out=E[:, ro_ + HP:ro_ + HP + n], in0=xT[:, c, s0:s0 + n], scalar=vcol("g_mix", c, 0),
                        in1=rbc[:, s0:s0 + n], op0=ALU.mult, op1=ALU.mult),
                        reads=[xb[c], rbcb, vecb, Eb[ei]], writes=[Eb[ei]])
                if p == 0:
                    a = regs[0][0] + HP + cfg.real_end0 - HP
                    S.op("act", lambda e, c=c, E=E, a=a: e.copy(out=phB[:, c, :], in_=E[:, a:a + HP]),
                         reads=[Eb[ei]], writes=[phBb[c]])
                else:
                    a = regs[0][0] + HP + regs[0][2] - HP
                    S.op("act", lambda e, c=c, E=E, a=a: e.copy(out=phB[:, c, :], in_=E[:, a:a + HP]),
                         reads=[Eb[ei]], writes=[phBb[c]])
                    a2 = regs[1][0] + HP + regs[1][2] - HP
                    S.op("act", lambda e, c=c, E=E, a2=a2: e.copy(out=phC[:, c, :], in_=E[:, a2:a2 + HP]),
                         reads=[Eb[ei]], writes=[phCb[c]])
                src, srcb = E, Eb[ei]
                for sidx in range(nsteps):
                    sh = 1 << sidx
                    lo = 2 * sh - 1
                    di = 2 + (sidx % 2)
                    dst, dstb = Et[di], Eb[di]
                    S.op("dve", lambda e, src=src, dst=dst, sh=sh, lo=lo: e.tensor_tensor(
                        out=dst[:, lo:NE], in0=src[:, lo:NE], in1=src[:, lo - sh:NE - sh], op=ALU.add),
                        reads=[srcb], writes=[dstb])
                    src, srcb = dst, dstb
                oi = 2 + (nsteps % 2)
                M, Mb = Et[oi], Eb[oi]
                for (ro_, s0, n) in regs:
                    a = ro_ + HP
                    S.op("dve", lambda e, src=src, M=M, a=a, n=n, s0=s0, g=g: e.tensor_tensor(
                        out=M[:, a:a + n], in0=src[:, a:a + n], in1=invt[:, g, s0:s0 + n], op=ALU.mult),
                        reads=[srcb, invb], writes=[Mb])
                    S.op("dve", lambda e, M=M, E=E, a=a, n=n, s0=s0, c=c: e.tensor_tensor(
                        out=hb[:, c, s0:s0 + n], in0=M[:, a:a + n], in1=E[:, a:a + n], op=ALU.subtract),
                        reads=[Mb, Eb[ei]], writes=[hbb[c]])
            if p == 1:
                S.op("sp", lambda e: e.dma_start(out=pool_out[0].rearrange("(c q) j -> q c j", q=128), in_=phB[:, :, :]),
                     reads=phBb, sem="outs", inc=16)
                S.op("sp", lambda e: e.dma_start(out=pool_out[1].rearrange("(c q) j -> q c j", q=128), in_=phC[:, :, :]),
                     reads=phCb, sem="outs", inc=16)
            for g in range(4):
                for j in range(KG):
                    oc = g * KG + j
                    s_ = stage("pool", KG * 128)
                    slot = mm_group([(ring[:, s_, kc * 128:(kc + 1) * 128], (lambda hh, g=g, kc=kc: hb[:, g * KG + kc, hh * H:(hh + 1) * H])) for kc in range(KG)],
                                    reads=[ringb[s_]] + hbb[g * KG:(g + 1) * KG])
                    ti = next_tmp()
                    S.op("act", lambda e, ti=ti, slot=slot, oc=oc: e.activation(
                        out=halves(tmp_ap(ti)), in_=psv(slot), func=AF.Identity, bias=bsc[:, oc:oc + 1],
                        scale=vcol("pool_scale", oc)),
                        reads=[psb[slot], bscb, vecb], writes=[tmpb[ti]])
                    S.op("dve", lambda e, ti=ti, oc=oc: e.tensor_tensor(out=xT[:, oc, :], in0=xT[:, oc, :], in1=tmp_ap(ti), op=ALU.add),
                         reads=[tmpb[ti], xb[oc]], writes=[xb[oc]])
            S.set_fence()

        def conv_mixer(p):
            segs = cfg.segs[p]
            S.set_fence()
            uo = R2o
            Ut = [scr[:, uo + i * NEC:uo + (i + 1) * NEC] for i in range(2)]
            Ub = [Buf() for _ in range(2)]
            bo = uo + 2 * NEC
            Ubf_all = scr[:, bo:bo + NEC].bitcast(BF16)
            Ubft = [Ubf_all[:, i * NEC:(i + 1) * NEC] for i in range(2)]
            Ubfb = [Buf() for _ in range(2)]
            mo = bo + NEC
            mean_t = scr[:, mo:mo + T]
            rstd_t = scr[:, mo + T:mo + 2 * T]
            meanb, rstdb = Buf(), Buf()
            regs = []
            ro = 0
            for (s0, n) in segs:
                regs.append((ro, s0, n))
                ro += HC + n
            NE = ro
            NEO = NE - HC
            H2 = NEO // 2
            assert NEO % 2 == 0 and H2 <= 512
            pieces = []
            for (ro_, s0, n) in regs:
                for hh in range(2):
                    lo_, hi_ = max(ro_, hh * H2), min(ro_ + n, (hh + 1) * H2)
                    if hi_ > lo_:
                        pieces.append((hh, lo_ - hh * H2, s0 + lo_ - ro_, hi_ - lo_))
            rms_to_hb("g_mix", 1)
            S.op("sp", lambda e: e.dma_start(out=xspill.rearrange("(c q) t -> q c t", q=128), in_=xT[:, :, :]),
                 reads=xb, writes=[spillb], sem="spill", inc=16)
            if p == 0:
                for i in range(2):
                    S.op("dve", lambda e, i=i: e.memset(Ut[i][:, 0:HC], 0.0), writes=[Ub[i]])

            def dwconv(c):
                ui = c % 2
                s_ = stage("dw", CW * 128)
                slot = next_slot()

                def fn(e, s_=s_, slot=slot, ui=ui):
                    ins = None
                    for hh in range(2):
                        for k in range(CW):
                            ins = e.matmul(ps[:, slot, hh, 0:H2], lhsT=ring[:, s_, k * 128:(k + 1) * 128],
                                           rhs=Ubft[ui][:, hh * H2 + k:hh * H2 + k + H2], start=(k == 0), stop=(k == CW - 1))
                    return ins
                S.op("pe", fn, reads=[ringb[s_], Ubfb[ui]], writes=[psb[slot]])
                for (hh, a_, xc, n) in pieces:
                    S.op("act", lambda e, c=c, slot=slot, hh=hh, a_=a_, xc=xc, n=n: e.activation(
                        out=xT[:, c, xc:xc + n], in_=ps[:, slot, hh, a_:a_ + n], func=AF.Identity, bias=vcol("b_dw", c)),
                        reads=[psb[slot], vecb], writes=[xb[c]])

            for c in range(KD):
                sv_ = stage("pw1", KD * 128)
                slotv = mm_group([(ring[:, sv_, kc * 128:(kc + 1) * 128], (lambda hh, kc=kc: hb[:, kc, hh * H:(hh + 1) * H])) for kc in range(KD)],
                                 reads=[ringb[sv_]] + hbb)
                sg_ = stage("pw1", KD * 128)
                slotg = mm_group([(ring[:, sg_, kc * 128:(kc + 1) * 128], (lambda hh, kc=kc: hb[:, kc, hh * H:(hh + 1) * H])) for kc in range(KD)],
                                 reads=[ringb[sg_]] + hbb)
                tv, tg = next_tmp(), next_tmp()
                S.op("dve", lambda e, tg=tg, slotg=slotg: e.tensor_tensor(
                    out=halves(tmp_ap(tg)), in0=psv(slotg), in1=halves(rbc[:, :]), op=ALU.mult),
                    reads=[psb[slotg], rbcb], writes=[tmpb[tg]])
                S.op("act", lambda e, tg=tg, c=c: e.activation(
                    out=tmp_ap(tg), in_=tmp_ap(tg), func=AF.Sigmoid, bias=vcol("b_pw1", KD + c)),
                    reads=[tmpb[tg], vecb], writes=[tmpb[tg]])
                S.op("dve", lambda e, tv=tv, slotv=slotv: e.tensor_tensor(
                    out=halves(tmp_ap(tv)), in0=psv(slotv), in1=halves(rbc[:, :]), op=ALU.mult),
                    reads=[psb[slotv], rbcb], writes=[tmpb[tv]])
                ui = c % 2
                U = Ut[ui]
                if p == 1:
                    S.op("act", lambda e, c=c, U=U: e.copy(out=U[:, regs[0][0]:regs[0][0] + HC], in_=uhB[:, c, :]),
                         reads=[uhBb[c]], writes=[Ub[ui]])
                    S.op("act", lambda e, c=c, U=U: e.copy(out=U[:, regs[1][0]:regs[1][0] + HC], in_=uhC[:, c, :]),
                         reads=[uhCb[c]], writes=[Ub[ui]])
                for (ro_, s0, n) in regs:
                    S.op("dve", lambda e, U=U, ro_=ro_, s0=s0, n=n, tv=tv, tg=tg, c=c: e.scalar_tensor_tensor(
                        out=U[:, ro_ + HC:ro_ + HC + n], in0=tmp_ap(tv)[:, s0:s0 + n], scalar=vcol("b_pw1", c),
                        in1=tmp_ap(tg)[:, s0:s0 + n], op0=ALU.add, op1=ALU.mult),
                        reads=[tmpb[tv], tmpb[tg], Ub[ui], vecb], writes=[Ub[ui]])
                if p == 0:
                    mo_ = cfg.voff["mask"]
                    S.op("dve", lambda e, U=U: e.tensor_scalar(out=U[:, HC:HC + HALO], in0=U[:, HC:HC + HALO],
                                                              scalar1=vecs[:, mo_:mo_ + 1], scalar2=None, op0=ALU.mult),
                         reads=[Ub[ui], vecb], writes=[Ub[ui]])
                    a = HC + cfg.real_end0 - HC
                    S.op("act", lambda e, c=c, U=U, a=a: e.copy(out=uhB[:, c, :], in_=U[:, a:a + HC]),
                         reads=[Ub[ui]], writes=[uhBb[c]])
                else:
                    a = regs[0][0] + HC + regs[0][2] - HC
                    S.op("act", lambda e, c=c, U=U, a=a: e.copy(out=uhB[:, c, :], in_=U[:, a:a + HC]),
                         reads=[Ub[ui]], writes=[uhBb[c]])
                    a2 = regs[1][0] + HC + regs[1][2] - HC
                    S.op("act", lambda e, c=c, U=U, a2=a2: e.copy(out=uhC[:, c, :], in_=U[:, a2:a2 + HC]),
                         reads=[Ub[ui]], writes=[uhCb[c]])
                S.op("act", lambda e, U=U, ui=ui: e.copy(out=Ubft[ui][:, 0:NE], in_=U[:, 0:NE]),
                     reads=[Ub[ui]], writes=[Ubfb[ui]])
                if c >= 1:
                    dwconv(c - 1)
            dwconv(KD - 1)
            if p == 1:
                S.op("sp", lambda e: e.dma_start(out=conv_out[0].rearrange("(c q) j -> q c j", q=128), in_=uhB[:, :, :]),
                     reads=uhBb, sem="outs", inc=16)
                S.op("sp", lambda e: e.dma_start(out=conv_out[1].rearrange("(c q) j -> q c j", q=128), in_=uhC[:, :, :]),
                     reads=uhCb, sem="outs", inc=16)
            s1, s2 = next_slot(), next_slot()
            acc1 = scr[:, mo + 2 * T:mo + 3 * T]
            acc2 = scr[:, mo + 3 * T:mo + 4 * T]
            acc1b, acc2b = Buf(), Buf()
            for c in range(KD):
                if c == 0:
                    S.op("act", lambda e: e.activation(out=acc2, in_=xT[:, 0, :], func=AF.Square), reads=[xb[0]], writes=[acc2b])
                    S.op("act", lambda e: e.copy(out=acc1, in_=xT[:, 0, :]), reads=[xb[0]], writes=[acc1b])
                else:
                    ti = next_tmp()
                    S.op("act", lambda e, c=c, ti=ti: e.activation(out=tmp_ap(ti), in_=xT[:, c, :], func=AF.Square),
                         reads=[xb[c]], writes=[tmpb[ti]])
                    S.op("dve", lambda e, c=c: e.tensor_tensor(out=acc1, in0=acc1, in1=xT[:, c, :], op=ALU.add),
                         reads=[xb[c], acc1b], writes=[acc1b])
                    S.op("dve", lambda e, ti=ti: e.tensor_tensor(out=acc2, in0=acc2, in1=tmp_ap(ti), op=ALU.add),
                         reads=[tmpb[ti], acc2b], writes=[acc2b])

            def fn(e):
                ins = None
                for hh in range(2):
                    e.matmul(ps[:, s1, hh, 0:H], lhsT=ones[:, :], rhs=acc1[:, hh * H:(hh + 1) * H], start=True, stop=True)
                    ins = e.matmul(ps[:, s2, hh, 0:H], lhsT=ones[:, :], rhs=acc2[:, hh * H:(hh + 1) * H], start=True, stop=True)
                return ins
            S.op("pe", fn, reads=[acc1b, acc2b, onesb], writes=[psb[s1], psb[s2]])
            S.op("act", lambda e: e.activation(out=halves(mean_t), in_=psv(s1), func=AF.Identity, scale=1.0 / D),
                 reads=[psb[s1]], writes=[meanb])
            tq = next_tmp()
            S.op("dve", lambda e: e.tensor_tensor(out=tmp_ap(tq), in0=mean_t, in1=mean_t, op=ALU.mult),
                 reads=[meanb], writes=[tmpb[tq]])
            S.op("dve", lambda e: e.scalar_tensor_tensor(out=halves(rstd_t), in0=psv(s2), scalar=1.0 / D, in1=halves(tmp_ap(tq)),
                                                         op0=ALU.mult, op1=ALU.subtract),
                 reads=[psb[s2], tmpb[tq]], writes=[rstdb])
            S.op("act", lambda e: e.activation(out=rstd_t, in_=rstd_t, func=AF.Sqrt, bias=EPS),
                 reads=[rstdb], writes=[rstdb])
            S.op("dve", lambda e: e.reciprocal(out=rstd_t, in_=rstd_t), reads=[rstdb], writes=[rstdb])
            for c in range(KD):
                t1, t2 = next_tmp(), next_tmp()
                S.op("dve", lambda e, c=c, t1=t1: e.tensor_tensor(out=tmp_ap(t1), in0=xT[:, c, :], in1=mean_t, op=ALU.subtract),
                     reads=[xb[c], meanb], writes=[tmpb[t1]])
                S.op("dve", lambda e, t1=t1, t2=t2: e.tensor_tensor(out=tmp_ap(t2), in0=tmp_ap(t1), in1=rstd_t, op=ALU.mult),
                     reads=[tmpb[t1], rstdb], writes=[tmpb[t2]])
                S.op("act", lambda e, c=c, t2=t2: e.activation(out=hb[:, c, :], in_=tmp_ap(t2), func=AF.Silu,
                                                              bias=vcol("ln_b", c), scale=vcol("ln_g", c)),
                     reads=[tmpb[t2], vecb], writes=[hbb[c]])
            S.op("sp", lambda e: e.dma_start(out=xT[:, :, :], in_=xspill.rearrange("(c q) t -> q c t", q=128)),
                 reads=[spillb], writes=xb, sem="spill", inc=16)
            for oc in range(KD):
                s_ = stage("pw2", KD * 128)
                slot = mm_group([(ring[:, s_, kc * 128:(kc + 1) * 128], (lambda hh, kc=kc: hb[:, kc, hh * H:(hh + 1) * H])) for kc in range(KD)],
                                reads=[ringb[s_]] + hbb)
                S.op("dve", lambda e, oc=oc, slot=slot: e.scalar_tensor_tensor(
                    out=halves(xT[:, oc, :]), in0=psv(slot), scalar=vcol("b_pw2", oc), in1=halves(xT[:, oc, :]),
                    op0=ALU.add, op1=ALU.add),
                    reads=[psb[slot], xb[oc], vecb], writes=[xb[oc]])
            S.set_fence()

        def ple(p, l):
            S.op("pool", lambda e: e.dma_start(out=pT[:, :, :], in_=pin[p * 2 + l].rearrange("(k q) t -> q k t", q=128)),
                 writes=[pTb], sem="pt", inc=16)
            rms_to_hb("g_ple", l)
            for c in range(KD):
                s_ = stage("ple", SW)
                slotg = mm_group([(ring[:, s_, kc * 128:(kc + 1) * 128], (lambda hh, kc=kc: hb[:, kc, hh * H:(hh + 1) * H])) for kc in range(KD)],
                                 reads=[ringb[s_]] + hbb)
                slotp = mm_group([(ring[:, s_, (KD + kc) * 128:(KD + kc + 1) * 128], (lambda hh, kc=kc: pT[:, kc, hh * H:(hh + 1) * H])) for kc in range(KP)],
                                 reads=[ringb[s_], pTb])
                tg, tm = next_tmp(), next_tmp()
                S.op("dve", lambda e, tg=tg, slotg=slotg: e.tensor_tensor(
                    out=halves(tmp_ap(tg)), in0=psv(slotg), in1=halves(rbc[:, :]), op=ALU.mult),
                    reads=[psb[slotg], rbcb], writes=[tmpb[tg]])
                S.op("act", lambda e, tg=tg: e.activation(out=tmp_ap(tg), in_=tmp_ap(tg), func=AF.Sigmoid),
                     reads=[tmpb[tg]], writes=[tmpb[tg]])
                S.op("dve", lambda e, tg=tg, tm=tm, slotp=slotp: e.tensor_tensor(
                    out=halves(tmp_ap(tm)), in0=psv(slotp), in1=halves(tmp_ap(tg)), op=ALU.mult),
                    reads=[psb[slotp], tmpb[tg]], writes=[tmpb[tm]])
                S.op("dve", lambda e, c=c, tm=tm: e.tensor_tensor(out=xT[:, c, :], in0=xT[:, c, :], in1=tmp_ap(tm), op=ALU.add),
                     reads=[tmpb[tm], xb[c]], writes=[xb[c]])

        for p in range(2):
            S.op("sp", lambda e, p=p: e.dma_start(out=xT[:, :, :], in_=xin[p].rearrange("(c q) t -> q c t", q=128)),
                 writes=xb, sem="x", inc=16)
            for l in range(2):
                rms_to_hb("g_ffn1", l)
                ffn(l, 1)
                if l == 0:
                    pool_mixer(p)
                else:
                    conv_mixer(p)
                rms_to_hb("g_ffn2", l)
                ffn(l, 2)
                ple(p, l)
            rms_stats()
            for c in range(KD):
                S.op("dve", lambda e, c=c: e.scalar_tensor_tensor(out=xT[:, c, :], in0=xT[:, c, :], scalar=vcol("g_final", c),
                                                                  in1=rbc[:, :], op0=ALU.mult, op1=ALU.mult),
                     reads=[xb[c], rbcb, vecb], writes=[xb[c]])
            S.op("sp", lambda e, p=p: e.dma_start(out=yout[p].rearrange("(c q) t -> q c t", q=128), in_=xT[:, :, :]),
                 reads=xb, sem="outy", inc=16)
        assert st["k"] == 2 * NS
        final_waits = [(k, S.count[k]) for k in ("outy", "outs", "spill") if k in S.count]

        with nc.Block() as block:
            @block.tensor
            def _(e):
                S.emit("pe", e, sems)

            @block.scalar
            def _(e):
                S.emit("act", e, sems)

            @block.vector
            def _(e):
                S.emit("dve", e, sems)

            @block.gpsimd
            def _(e):
                S.emit("pool", e, sems)

            @block.sync
            def _(e):
                S.emit("sp", e, sems)
                for k, v in final_waits:
                    e.wait_ge(sems[k], v)
    return nc


def _tile_w(W, KD):
    K, N = W.shape
    return np.ascontiguousarray(W.reshape(KD, 128, N // 128, 128).transpose(2, 1, 0, 3)).reshape(N // 128, 128, KD * 128)


def _vec_cols(v):
    v = np.asarray(v, np.float32).reshape(-1)
    return np.ascontiguousarray(v.reshape(-1, 128).T)


def build_w_all(cfg, inp):
    plan = stage_plan(cfg)
    NS = len(plan)
    KD, KF, KP, KG = cfg.KD, cfg.KF, cfg.KP, cfg.KG
    W = np.zeros((NS, 128, cfg.SW), np.float32)
    idx = {}
    for k, e in enumerate(plan):
        if e[0] in ("g", "u", "d"):
            key = (e[0], e[1], e[2])
        elif e[0] == "ple":
            key = ("ple", e[1])
        else:
            key = (e[0],)
        idx.setdefault(key, []).append(k)
    for l in range(2):
        for which in (1, 2):
            names = {1: ("ffn1_w_gate", "ffn1_w_up", "ffn1_w_down"), 2: ("ffn2_w_gate", "ffn2_w_up", "ffn2_w_down")}[which]
            wg, wu, wd = inp[names[0]][l], inp[names[1]][l], inp[names[2]][l]
            W[idx[("g", l, which)], :, :KD * 128] = _tile_w(wg, KD)
            W[idx[("u", l, which)], :, :KD * 128] = _tile_w(wu, KD)
            W[idx[("d", l, which)], :, :cfg.D] = wd.reshape(KF, 128, cfg.D)
    pk = idx[("pool",)]
    pw = inp["pool_w"][0]
    for g in range(4):
        W[pk[g * KG:(g + 1) * KG], :, :KG * 128] = _tile_w(pw[g], KG)
    t1 = _tile_w(inp["conv_w_pw1"][0], KD)
    order = []
    for c in range(KD):
        order += [c, KD + c]
    W[idx[("pw1",)], :, :KD * 128] = t1[order]
    W[idx[("pw2",)], :, :KD * 128] = _tile_w(inp["conv_w_pw2"][0], KD)
    wdw = inp["conv_w_dw"][0].reshape(CW, KD, 128)
    dwt = np.zeros((KD, 128, CW * 128), np.float32)
    qq = np.arange(128)
    for k in range(CW):
        dwt[:, qq, k * 128 + qq] = wdw[k]
    W[idx[("dw",)], :, :CW * 128] = dwt
    for l in range(2):
        kk = idx[("ple", l)]
        W[kk, :, :KD * 128] = _tile_w(inp["ple_w_gate"][l], KD)
        W[kk, :, KD * 128:(KD + KP) * 128] = _tile_w(inp["ple_w_proj"][l], KP)
    return W


def prep_inputs(cfg, inp):
    D, T, KD = cfg.D, cfg.T, cfg.KD
    inp = {k: np.asarray(v) for k, v in inp.items()}
    w_all = build_w_all(cfg, inp)
    vecs = np.zeros((128, cfg.NV), np.float32)
    for l in range(2):
        for nm in ("g_ffn1", "g_mix", "g_ffn2", "g_ple"):
            o = cfg.voff[(nm, l)]
            vecs[:, o:o + KD] = _vec_cols(inp[nm][l])
    for nm, src in (("g_final", inp["g_final"]), ("pool_b", inp["pool_b"][0]), ("pool_scale", inp["pool_scale"][0]),
                    ("b_pw1", inp["conv_b_pw1"][0]), ("b_dw", inp["conv_b_dw"][0]), ("ln_g", inp["conv_ln_g"][0]),
                    ("ln_b", inp["conv_ln_b"][0]), ("b_pw2", inp["conv_b_pw2"][0])):
        o = cfg.voff[nm]
        vc = _vec_cols(src)
        vecs[:, o:o + vc.shape[1]] = vc
    xp, xs = inp["x_prompt"], inp["x_sample"]
    pp, psm = inp["p_prompt"], inp["p_sample"]
    in_maps = []
    for c in range(N_CORES):
        b, q = c // cfg.CPB, c % cfg.CPB
        s = q * cfg.NP
        xin = np.zeros((2, T, D), np.float32)
        pin = np.zeros((2, 2, T, cfg.PLE), np.float32)
        lo = s - HALO
        src_lo = max(lo, 0)
        n_real = s + cfg.OWN1 - src_lo
        xin[0, src_lo - lo:src_lo - lo + n_real] = xp[b, src_lo:s + cfg.OWN1]
        pin[0, :, src_lo - lo:src_lo - lo + n_real] = pp[:, b, src_lo:s + cfg.OWN1]
        xin[1, :cfg.OWN2] = xp[b, s + cfg.OWN1:s + cfg.NP]
        pin[1, :, :cfg.OWN2] = pp[:, b, s + cfg.OWN1:s + cfg.NP]
        xin[1, cfg.OWN2:] = xs[c]
        pin[1, :, cfg.OWN2:] = psm[:, c]
        pos = np.zeros((2, T), np.int64)
        pos[0] = lo + np.arange(T)
        pos[1, :cfg.OWN2] = s + cfg.OWN1 + np.arange(cfg.OWN2)
        pos[1, cfg.OWN2:] = cfg.PAST_LEN + np.arange(cfg.DEC_SEQ)
        inv = np.zeros((2, 4, T), np.float32)
        for g, win in enumerate(POOL_WINDOWS):
            cnt = np.minimum(np.maximum(pos, 0) + 1, win).astype(np.float32)
            inv[:, g, :] = (np.float32(1.0) / cnt).astype(np.float32)
        invc = np.ascontiguousarray(np.broadcast_to(inv.reshape(2, 1, 4 * T), (2, 128, 4 * T)))
        vc = vecs.copy()
        vc[:, cfg.voff["mask"]] = 0.0 if q == 0 else 1.0
        in_maps.append({
            "xin": np.ascontiguousarray(xin.transpose(0, 2, 1)),
            "pin": np.ascontiguousarray(pin.transpose(0, 1, 3, 2)).reshape(4, cfg.PLE, T),
            "invc": invc,
            "spool": np.ascontiguousarray(inp["state_pool"][0, c].T),
            "sconv": np.ascontiguousarray(inp["state_conv"][0, c].T),
            "vecs": vc,
            "w_all": w_all,
        })
    return in_maps


def assemble(cfg, results):
    D = cfg.D
    y_prompt = np.zeros((cfg.BATCH, cfg.SEQ, D), np.float32)
    y_sample = np.zeros((cfg.DEC_BATCH, cfg.DEC_SEQ, D), np.float32)
    npp = np.zeros((1, cfg.BATCH, HP, D), np.float32)
    ncp = np.zeros((1, cfg.BATCH, HC, D), np.float32)
    nps = np.zeros((1, cfg.DEC_BATCH, HP, D), np.float32)
    ncs = np.zeros((1, cfg.DEC_BATCH, HC, D), np.float32)
    for c in range(N_CORES):
        r = results[c]
        b, q = c // cfg.CPB, c % cfg.CPB
        s = q * cfg.NP
        yo = r["yout"]
        y_prompt[b, s:s + cfg.OWN1] = yo[0][:, HALO:HALO + cfg.OWN1].T
        y_prompt[b, s + cfg.OWN1:s + cfg.NP] = yo[1][:, :cfg.OWN2].T
        y_sample[c] = yo[1][:, cfg.OWN2:].T
        nps[0, c] = r["pool_out"][1].T
        ncs[0, c] = r["conv_out"][1].T
        if q == cfg.CPB - 1:
            npp[0, b] = r["pool_out"][0].T
            ncp[0, b] = r["conv_out"][0].T
    return (y_prompt, y_sample, npp, ncp, nps, ncs)


def run(cfg, inputs, trace=False):
    nc = build_program(cfg)
    in_maps = prep_inputs(cfg, inputs)
    res = run_bass_kernel_spmd(nc, in_maps, core_ids=list(range(N_CORES)), trace=trace)
    return assemble(cfg, res.results), res


def kernel(**inputs):
    cfg = Cfg()
    outs, _ = run(cfg, inputs)
    return outs
```

```python
import numpy as np
import contextlib
import concourse.bass as bass
import concourse.mybir as mybir
from concourse.bass_utils import run_bass_kernel_spmd

F32 = mybir.dt.float32
BF16 = mybir.dt.bfloat16
AF = mybir.ActivationFunctionType
ALU = mybir.AluOpType

N_CORES = 8
POOL_WINDOWS = (2, 4, 8, 16)
HP = 15
HC = 30
CW = 31
HALO = 45


class Cfg:
    def __init__(self, D=4096, F=11008, PLE=256, SEQ=4096, BATCH=2, DEC_BATCH=8, DEC_SEQ=16,
                 PAST_LEN=1024, T=544, OWN1=496, NST=7, G=4):
        self.D, self.F, self.PLE, self.SEQ, self.BATCH = D, F, PLE, SEQ, BATCH
        self.DEC_BATCH, self.DEC_SEQ, self.PAST_LEN = DEC_BATCH, DEC_SEQ, PAST_LEN
        self.T, self.OWN1, self.NST, self.G = T, OWN1, NST, G
        self.KD, self.KF, self.KP = D // 128, F // 128, PLE // 128
        assert D % 512 == 0 and F % 128 == 0 and PLE % 128 == 0
        self.KG = self.KD // 4
        self.CPB = N_CORES // BATCH
        self.NP = SEQ // self.CPB
        self.OWN2 = T - DEC_SEQ
        assert self.OWN1 + self.OWN2 == self.NP, (self.OWN1, self.OWN2, self.NP)
        self.PAD1 = T - HALO - OWN1
        assert self.PAD1 >= 0 and T % 2 == 0
        assert DEC_BATCH == N_CORES
        self.H = T // 2
        self.SW = max((self.KD + self.KP) * 128, CW * 128)
        self.EPS = 1e-6
        KD = self.KD
        off = {}
        o = 0
        for l in range(2):
            for nm in ("g_ffn1", "g_mix", "g_ffn2", "g_ple"):
                off[(nm, l)] = o
                o += KD
        for nm, w in (("g_final", KD), ("pool_b", KD), ("pool_scale", KD), ("b_pw1", 2 * KD), ("b_dw", KD),
                      ("ln_g", KD), ("ln_b", KD), ("b_pw2", KD), ("mask", 1)):
            off[nm] = o
            o += w
        self.voff = off
        self.NV = o
        self.segs = [[(0, T)], [(0, self.OWN2), (self.OWN2, DEC_SEQ)]]
        self.real_end0 = HALO + OWN1


def stage_plan(cfg):
    plan = []
    KD, KF, G = cfg.KD, cfg.KF, cfg.G
    for l in range(2):
        def ffn(which):
            for g0 in range(0, KF, G):
                grp = list(range(g0, min(g0 + G, KF)))
                for j in grp:
                    plan.append(("g", l, which, j))
                    plan.append(("u", l, which, j))
                for j in grp:
                    plan.append(("d", l, which, j))
        ffn(1)
        if l == 0:
            for g in range(4):
                for j in range(cfg.KG):
                    plan.append(("pool", g, j))
        else:
            for c in range(KD):
                plan.append(("pw1", c))
                plan.append(("pw1", KD + c))
                if c >= 1:
                    plan.append(("dw", c - 1))
            plan.append(("dw", KD - 1))
            for c in range(KD):
                plan.append(("pw2", c))
        ffn(2)
        for c in range(KD):
            plan.append(("ple", l, c))
    return plan


class Buf:
    __slots__ = ("w", "r", "name")

    def __init__(self, name=""):
        self.w = None
        self.r = {}
        self.name = name


ENG_SEM = {"pe": "pe", "act": "act", "dve": "dve"}


class Sched:
    def __init__(self):
        self.ops = {e: [] for e in ("pe", "act", "dve", "pool", "sp")}
        self.count = {}
        self.waited = {e: {} for e in self.ops}
        self.fence = {}

    def set_fence(self):
        self.fence = {k: self.count.get(k, 0) for k in ("pe", "act", "dve")}

    def op(self, eng, fn, reads=(), writes=(), sem=None, inc=1, deps=()):
        need = {}

        def add(k, v):
            if v and need.get(k, 0) < v:
                need[k] = v

        for b in reads:
            if b.w is not None:
                add(*b.w)
        for b in writes:
            if b.w is not None:
                add(*b.w)
            for k, v in b.r.items():
                add(k, v)
        for t in deps:
            if t is not None:
                add(*t)
        if eng in ("pe", "act", "dve"):
            for k, v in self.fence.items():
                add(k, v)
        if eng == "pe":
            need.pop("pe", None)
        wd = self.waited[eng]
        waits = []
        for k, v in need.items():
            if wd.get(k, 0) < v:
                wd[k] = v
                waits.append((k, v))
        semk = sem or ENG_SEM[eng]
        self.count[semk] = self.count.get(semk, 0) + inc
        tok = (semk, self.count[semk])
        self.ops[eng].append((waits, fn, (semk, inc)))
        for b in reads:
            if b.r.get(semk, 0) < tok[1]:
                b.r[semk] = tok[1]
        for b in writes:
            b.w = tok
            b.r = {}
        return tok

    def emit(self, eng, e, sems):
        for waits, fn, (semk, inc) in self.ops[eng]:
            for k, v in waits:
                e.wait_ge(sems[k], v)
            ins = fn(e)
            ins.then_inc(sems[semk], inc)


def build_program(cfg):
    D, KD, KF, KP, KG, T, H, G, NST, SW = cfg.D, cfg.KD, cfg.KF, cfg.KP, cfg.KG, cfg.T, cfg.H, cfg.G, cfg.NST, cfg.SW
    EPS = cfg.EPS
    plan = stage_plan(cfg)
    NS = len(plan)
    nc = bass.Bass("TRN2", target_bir_lowering=False)

    def din(name, shape):
        return nc.dram_tensor(name, list(shape), F32, kind="ExternalInput").ap()

    def dout(name, shape):
        return nc.dram_tensor(name, list(shape), F32, kind="ExternalOutput").ap()

    xin = din("xin", [2, D, T])
    pin = din("pin", [2 * 2, KP * 128, T])
    invc = din("invc", [2, 128, 4 * T])
    spool = din("spool", [D, HP])
    sconv = din("sconv", [D, HC])
    vecs_d = din("vecs", [128, cfg.NV])
    w_all = din("w_all", [NS, 128, SW])
    yout = dout("yout", [2, D, T])
    pool_out = dout("pool_out", [2, D, HP])
    conv_out = dout("conv_out", [2, D, HC])
    xspill = dout("xspill", [D, T])

    NTMP = 4
    NEP = T + 2 * HP
    NEC = T + 2 * HC
    r2_ffn = G * T
    r2_pool = 4 * T + 3 * NEP
    r2_conv = 3 * NEC + 4 * T
    R2 = max(r2_ffn, r2_pool, r2_conv)
    SCR = NTMP * T + R2

    S = Sched()
    with contextlib.ExitStack() as es:
        def sb(name, shape, dt):
            return es.enter_context(nc.sbuf_tensor("sb_" + name, list(shape), dt))

        xT = sb("xT", [128, KD, T], F32)
        hb = sb("hb", [128, KD, T], BF16)
        ring = sb("ring", [128, NST, SW], BF16)
        scr = sb("scr", [128, SCR], F32)
        rbc = sb("rbc", [128, T], F32)
        sacc = sb("sacc", [128, T], F32)
        uhB = sb("uhB", [128, KD, HC], F32)
        uhC = sb("uhC", [128, KD, HC], F32)
        phB = sb("phB", [128, KD, HP], F32)
        phC = sb("phC", [128, KD, HP], F32)
        vecs = sb("vecs", [128, cfg.NV], F32)
        bsc = sb("bsc", [128, KD], F32)
        pT = sb("pT", [128, KP, T], BF16)
        ones = sb("ones", [128, 128], F32)
        ps = es.enter_context(nc.psum_tensor("ps", [128, 4, 2, 512], F32))

        sem_names = ["pe", "act", "dve", "x", "pt", "vec", "hist", "invc", "spill", "outy", "outs"] + \
                    ["ring%d" % i for i in range(NST)]
        sems = {n: es.enter_context(nc.semaphore(n)) for n in sem_names}

        xb = [Buf("x%d" % c) for c in range(KD)]
        hbb = [Buf("hb%d" % c) for c in range(KD)]
        psb = [Buf("ps%d" % i) for i in range(4)]
        ringb = [Buf("ring%d" % i) for i in range(NST)]
        tmpb = [Buf("tmp%d" % i) for i in range(NTMP)]
        rbcb = Buf("rbc")
        saccb = Buf("sacc")
        vecb = Buf("vecs")
        bscb = Buf("bsc")
        onesb = Buf("ones")
        pTb = Buf("pT")
        uhBb = [Buf() for _ in range(KD)]
        uhCb = [Buf() for _ in range(KD)]
        phBb = [Buf() for _ in range(KD)]
        phCb = [Buf() for _ in range(KD)]
        spillb = Buf("spill")
        r2b = {}

        def tmp_ap(i):
            return scr[:, i * T:(i + 1) * T]

        R2o = NTMP * T
        st = {"slot": 0, "tmp": 0, "k": 0}

        def next_slot():
            s = st["slot"]
            st["slot"] = (s + 1) % 4
            return s

        def next_tmp():
            i = st["tmp"]
            st["tmp"] = (i + 1) % NTMP
            return i

        def halves(ap_flat):
            return ap_flat.rearrange("p (a h) -> p a h", a=2)

        def psv(slot):
            return ps[:, slot, :, 0:H]

        def vcol(name, c, l=None):
            o = cfg.voff[(name, l)] if l is not None else cfg.voff[name]
            return vecs[:, o + c:o + c + 1]

        def stage(kind, ncols):
            k = st["k"]
            st["k"] = k + 1
            kk = k % NS
            assert plan[kk][0] == kind, (plan[kk], kind)
            s = k % NST
            S.op("pool", lambda e, s=s, kk=kk, ncols=ncols: e.dma_start(out=ring[:, s, 0:ncols], in_=w_all[kk, :, 0:ncols]),
                 writes=[ringb[s]], sem="ring%d" % s, inc=16)
            return s

        def mm_group(pairs, reads):
            slot = next_slot()

            def fn(e, pairs=pairs, slot=slot):
                n = len(pairs)
                ins = None
                for i, (lhsT, rhs) in enumerate(pairs):
                    for hh in range(2):
                        ins = e.matmul(ps[:, slot, hh, 0:H], lhsT=lhsT, rhs=rhs(hh),
                                       start=(i == 0), stop=(i == n - 1))
                return ins
            S.op("pe", fn, reads=reads, writes=[psb[slot]])
            if st.get("pending"):
                st["pending"] -= 1
                if st["pending"] == 0:
                    rms_finish()
                    dl = st.get("deferred") or []
                    st["deferred"] = []
                    for f in dl:
                        f()
            return slot

        def evac_or_defer(f):
            if st.get("pending"):
                st.setdefault("deferred", []).append(f)
            else:
                f()

        S.op("sp", lambda e: e.dma_start(out=vecs[:, :], in_=vecs_d[:, :]), writes=[vecb], sem="vec", inc=16)
        S.op("sp", lambda e: e.dma_start(out=phC[:, :, :], in_=spool.rearrange("(c q) j -> q c j", q=128)),
             writes=phCb, sem="hist", inc=16)
        S.op("sp", lambda e: e.dma_start(out=uhC[:, :, :], in_=sconv.rearrange("(c q) j -> q c j", q=128)),
             writes=uhCb, sem="hist", inc=16)
        for b in phCb + uhCb:
            b.w = ("hist", 32)
        S.op("dve", lambda e: e.memset(ones[:, :], 1.0), writes=[onesb])
        po, pso = cfg.voff["pool_b"], cfg.voff["pool_scale"]
        S.op("dve", lambda e: e.tensor_tensor(out=bsc[:, :], in0=vecs[:, po:po + KD], in1=vecs[:, pso:pso + KD], op=ALU.mult),
             reads=[vecb], writes=[bscb])

        nxt = {"g": None, "l": None, "inline": False}

        def set_next(g, l=None, inline=False):
            nxt["g"], nxt["l"], nxt["inline"] = g, l, inline

        def stats_chunk(c):
            if c == 0:
                S.op("act", lambda e: e.activation(out=sacc[:, :], in_=xT[:, 0, :], func=AF.Square),
                     reads=[xb[0]], writes=[saccb])
            else:
                ti = next_tmp()
                S.op("act", lambda e, c=c, ti=ti: e.activation(out=tmp_ap(ti), in_=xT[:, c, :], func=AF.Square),
                     reads=[xb[c]], writes=[tmpb[ti]])
                S.op("dve", lambda e, ti=ti: e.tensor_tensor(out=sacc[:, :], in0=sacc[:, :], in1=tmp_ap(ti), op=ALU.add),
                     reads=[tmpb[ti], saccb], writes=[saccb])

        def hb_chunk(c, eng):
            g_, l_ = nxt["g"], nxt["l"]
            if eng == "act":
                S.op("act", lambda e, c=c, g_=g_, l_=l_: e.activation(out=hb[:, c, :], in_=xT[:, c, :], func=AF.Identity,
                                                                     scale=vcol(g_, c, l_)),
                     reads=[xb[c], vecb], writes=[hbb[c]])
            else:
                S.op("dve", lambda e, c=c, g_=g_, l_=l_: e.tensor_scalar(out=hb[:, c, :], in0=xT[:, c, :], scalar1=vcol(g_, c, l_),
                                                                        scalar2=None, op0=ALU.mult),
                     reads=[xb[c], vecb], writes=[hbb[c]])

        def post_x(c):
            if nxt["inline"]:
                if nxt["g"] is not None:
                    hb_chunk(c, "act")
            else:
                stats_chunk(c)

        def rms_finish():
            slot = next_slot()

            def fn(e, slot=slot):
                ins = None
                for hh in range(2):
                    ins = e.matmul(ps[:, slot, hh, 0:H], lhsT=ones[:, :], rhs=sacc[:, hh * H:(hh + 1) * H],
                                   start=True, stop=True)
                return ins
            S.op("pe", fn, reads=[saccb, onesb], writes=[psb[slot]])
            S.op("act", lambda e, slot=slot: e.activation(out=halves(rbc[:, :]), in_=psv(slot), func=AF.Sqrt,
                                                          bias=EPS, scale=1.0 / D),
                 reads=[psb[slot]], writes=[rbcb])
            S.op("dve", lambda e: e.reciprocal(out=rbc[:, :], in_=rbc[:, :]), reads=[rbcb], writes=[rbcb])

        def norm_close():
            if nxt["inline"]:
                for c in range(KD):
                    stats_chunk(c)
            elif nxt["g"] is not None:
                for c in range(KD):
                    hb_chunk(c, "act" if c % 2 == 0 else "dve")
            if nxt["g"] is None:
                rms_finish()
            else:
                st["pending"] = 3 if nxt["g"] in ("g_ffn1", "g_ffn2") else 1

        def ffn(l, which):
            act = scr[:, R2o:R2o + G * T].bitcast(BF16).rearrange("p (a g t) -> p a g t", a=2, g=G)
            actb = [[Buf() for _ in range(G)] for _ in range(2)]
            gi = 0
            for g0 in range(0, KF, G):
                grp = list(range(g0, min(g0 + G, KF)))
                par = gi % 2
                gi += 1
                for jj, j in enumerate(grp):
                    sg_ = stage("g", KD * 128)
                    slotg = mm_group([(ring[:, sg_, kc * 128:(kc + 1) * 128], (lambda hh, kc=kc: hb[:, kc, hh * H:(hh + 1) * H])) for kc in range(KD)],
                                     reads=[ringb[sg_]] + hbb)
                    su_ = stage("u", KD * 128)
                    slotu = mm_group([(ring[:, su_, kc * 128:(kc + 1) * 128], (lambda hh, kc=kc: hb[:, kc, hh * H:(hh + 1) * H])) for kc in range(KD)],
                                     reads=[ringb[su_]] + hbb)
                    def ev(slotg=slotg, slotu=slotu, par=par, jj=jj):
                        ta, tb_ = next_tmp(), next_tmp()
                        S.op("dve", lambda e, ta=ta, slotg=slotg: e.tensor_tensor(
                            out=halves(tmp_ap(ta)), in0=psv(slotg), in1=halves(rbc[:, :]), op=ALU.mult),
                            reads=[psb[slotg], rbcb], writes=[tmpb[ta]])
                        S.op("act", lambda e, ta=ta: e.activation(out=tmp_ap(ta), in_=tmp_ap(ta), func=AF.Silu),
                             reads=[tmpb[ta]], writes=[tmpb[ta]])
                        S.op("dve", lambda e, tb_=tb_, slotu=slotu: e.tensor_tensor(
                            out=halves(tmp_ap(tb_)), in0=psv(slotu), in1=halves(rbc[:, :]), op=ALU.mult),
                            reads=[psb[slotu], rbcb], writes=[tmpb[tb_]])
                        S.op("dve", lambda e, ta=ta, tb_=tb_, par=par, jj=jj: e.tensor_tensor(
                            out=act[:, par, jj, :], in0=tmp_ap(tb_), in1=tmp_ap(ta), op=ALU.mult),
                            reads=[tmpb[ta], tmpb[tb_]], writes=[actb[par][jj]])
                    evac_or_defer(ev)
                ds = [stage("d", D) for _ in grp]
                for c in range(KD):
                    slot = mm_group([(ring[:, ds[jj], c * 128:(c + 1) * 128], (lambda hh, par=par, jj=jj: act[:, par, jj, hh * H:(hh + 1) * H])) for jj in range(len(grp))],
                                    reads=[ringb[s_] for s_ in ds] + actb[par][:len(grp)])
                    S.op("dve", lambda e, c=c, slot=slot: e.scalar_tensor_tensor(
                        out=halves(xT[:, c, :]), in0=psv(slot), scalar=0.5, in1=halves(xT[:, c, :]), op0=ALU.mult, op1=ALU.add),
                        reads=[psb[slot], xb[c]], writes=[xb[c]])
                    if grp[-1] == KF - 1:
                        post_x(c)
            norm_close()

        def pool_mixer(p):
            segs = cfg.segs[p]
            S.set_fence()
            invt = scr[:, R2o:R2o + 4 * T].rearrange("p (g t) -> p g t", g=4)
            invb = Buf()
            S.op("sp", lambda e: e.dma_start(out=scr[:, R2o:R2o + 4 * T], in_=invc[p, :, :]), writes=[invb], sem="invc", inc=16,
                 deps=[(k, v) for k, v in S.fence.items()])
            eo = R2o + 4 * T
            Et = [scr[:, eo + i * NEP:eo + (i + 1) * NEP] for i in (0, 0, 1, 2)]
            Eb = [Buf() for _ in range(4)]
            Eb[1] = Eb[0]
            regs = []
            ro = 0
            for (s0, n) in segs:
                regs.append((ro, s0, n))
                ro += HP + n
            NE = ro
            set_next("g_ffn2", 0, inline=False)
            if p == 0:
                S.op("dve", lambda e: e.memset(Et[0][:, 0:HP], 0.0), writes=[Eb[0]])
            for c in range(KD):
                g = c // KG
                nsteps = g + 1
                ei = c % 2
                E = Et[ei]
                if p == 1:
                    S.op("act", lambda e, c=c, E=E: e.copy(out=E[:, regs[0][0]:regs[0][0] + HP], in_=phB[:, c, :]),
                         reads=[phBb[c]], writes=[Eb[ei]])
                    S.op("act", lambda e, c=c, E=E: e.copy(out=E[:, regs[1][0]:regs[1][0] + HP], in_=phC[:, c, :]),
                         reads=[phCb[c]], writes=[Eb[ei]])
                for (ro_, s0, n) in regs:
                    S.op("dve", lambda e, c=c, E=E, ro_=ro_, s0=s0, n=n: e.scalar_tensor_tensor(
                        out=E[:, ro_ + HP:ro_ + HP + n], in0=xT[:, c, s0:s0 + n], scalar=vcol("g_mix", c, 0),
                        in1=rbc[:, s0:s0 + n], op0=ALU.mult, op1=ALU.mult),
                        reads=[xb[c], rbcb, vecb, Eb[ei]], writes=[Eb[ei]])
                if p == 0:
                    a = regs[0][0] + HP + cfg.real_end0 - HP
                    S.op("act", lambda e, c=c, E=E, a=a: e.copy(out=phB[:, c, :], in_=E[:, a:a + HP]),
                         reads=[Eb[ei]], writes=[phBb[c]])
                else:
                    a = regs[0][0] + HP + regs[0][2] - HP
                    S.op("act", lambda e, c=c, E=E, a=a: e.copy(out=phB[:, c, :], in_=E[:, a:a + HP]),
                         reads=[Eb[ei]], writes=[phBb[c]])
                    a2 = regs[1][0] + HP + regs[1][2] - HP
                    S.op("act", lambda e, c=c, E=E, a2=a2: e.copy(out=phC[:, c, :], in_=E[:, a2:a2 + HP]),
                         reads=[Eb[ei]], writes=[phCb[c]])
                src, srcb = E, Eb[ei]
                for sidx in range(nsteps):
                    sh = 1 << sidx
                    lo = 2 * sh - 1
                    di = 2 + (sidx % 2)
                    dst, dstb = Et[di], Eb[di]
                    S.op("dve", lambda e, src=src, dst=dst, sh=sh, lo=lo: e.tensor_tensor(
                        out=dst[:, lo:NE], in0=src[:, lo:NE], in1=src[:, lo - sh:NE - sh], op=ALU.add),
                        reads=[srcb], writes=[dstb])
                    src, srcb = dst, dstb
                oi = 2 + (nsteps % 2)
                M, Mb = Et[oi], Eb[oi]
                for (ro_, s0, n) in regs:
                    a = ro_ + HP
                    S.op("dve", lambda e, src=src, M=M, a=a, n=n, s0=s0, g=g: e.tensor_tensor(
                        out=M[:, a:a + n], in0=src[:, a:a + n], in1=invt[:, g, s0:s0 + n], op=ALU.mult),
                        reads=[srcb, invb], writes=[Mb])
                    S.op("dve", lambda e, M=M, E=E, a=a, n=n, s0=s0, c=c: e.tensor_tensor(
                        out=hb[:, c, s0:s0 + n], in0=M[:, a:a + n], in1=E[:, a:a + n], op=ALU.subtract),
                        reads=[Mb, Eb[ei]], writes=[hbb[c]])
            if p == 1:
                S.op("sp", lambda e: e.dma_start(out=pool_out[0].rearrange("(c q) j -> q c j", q=128), in_=phB[:, :, :]),
                     reads=phBb, sem="outs", inc=16)
                S.op("sp", lambda e: e.dma_start(out=pool_out[1].rearrange("(c q) j -> q c j", q=128), in_=phC[:, :, :]),
                     reads=phCb, sem="outs", inc=16)
            for g in range(4):
                for j in range(KG):
                    oc = g * KG + j
                    s_ = stage("pool", KG * 128)
                    slot = mm_group([(ring[:, s_, kc * 128:(kc + 1) * 128], (lambda hh, g=g, kc=kc: hb[:, g * KG + kc, hh * H:(hh + 1) * H])) for kc in range(KG)],
                                    reads=[ringb[s_]] + hbb[g * KG:(g + 1) * KG])
                    ti = next_tmp()
                    S.op("act", lambda e, ti=ti, slot=slot, oc=oc: e.activation(
                        out=halves(tmp_ap(ti)), in_=psv(slot), func=AF.Identity, bias=bsc[:, oc:oc + 1],
                        scale=vcol("pool_scale", oc)),
                        reads=[psb[slot], bscb, vecb], writes=[tmpb[ti]])
                    S.op("dve", lambda e, ti=ti, oc=oc: e.tensor_tensor(out=xT[:, oc, :], in0=xT[:, oc, :], in1=tmp_ap(ti), op=ALU.add),
                         reads=[tmpb[ti], xb[oc]], writes=[xb[oc]])
                    post_x(oc)
            norm_close()
            S.set_fence()

        def conv_mixer(p):
            segs = cfg.segs[p]
            S.set_fence()
            uo = R2o
            Ut = [scr[:, uo + i * NEC:uo + (i + 1) * NEC] for i in range(2)]
            Ub = [Buf() for _ in range(2)]
            bo = uo + 2 * NEC
            Ubf_all = scr[:, bo:bo + NEC].bitcast(BF16)
            Ubft = [Ubf_all[:, i * NEC:(i + 1) * NEC] for i in range(2)]
            Ubfb = [Buf() for _ in range(2)]
            mo = bo + NEC
            mean_t = scr[:, mo:mo + T]
            rstd_t = scr[:, mo + T:mo + 2 * T]
            meanb, rstdb = Buf(), Buf()
            regs = []
            ro = 0
            for (s0, n) in segs:
                regs.append((ro, s0, n))
                ro += HC + n
            NE = ro
            NEO = NE - HC
            H2 = NEO // 2
            assert NEO % 2 == 0 and H2 <= 512
            pieces = []
            for (ro_, s0, n) in regs:
                for hh in range(2):
                    lo_, hi_ = max(ro_, hh * H2), min(ro_ + n, (hh + 1) * H2)
                    if hi_ > lo_:
                        pieces.append((hh, lo_ - hh * H2, s0 + lo_ - ro_, hi_ - lo_))
            S.op("sp", lambda e: e.dma_start(out=xspill.rearrange("(c q) t -> q c t", q=128), in_=xT[:, :, :]),
                 reads=xb, writes=[spillb], sem="spill", inc=16)
            if p == 0:
                for i in range(2):
                    S.op("dve", lambda e, i=i: e.memset(Ut[i][:, 0:HC], 0.0), writes=[Ub[i]])

            def dwconv(c):
                ui = c % 2
                s_ = stage("dw", CW * 128)
                slot = next_slot()

                def fn(e, s_=s_, slot=slot, ui=ui):
                    ins = None
                    for hh in range(2):
                        for k in range(CW):
                            ins = e.matmul(ps[:, slot, hh, 0:H2], lhsT=ring[:, s_, k * 128:(k + 1) * 128],
                                           rhs=Ubft[ui][:, hh * H2 + k:hh * H2 + k + H2], start=(k == 0), stop=(k == CW - 1))
                    return ins
                S.op("pe", fn, reads=[ringb[s_], Ubfb[ui]], writes=[psb[slot]])
                for (hh, a_, xc, n) in pieces:
                    S.op("act", lambda e, c=c, slot=slot, hh=hh, a_=a_, xc=xc, n=n: e.activation(
                        out=xT[:, c, xc:xc + n], in_=ps[:, slot, hh, a_:a_ + n], func=AF.Identity, bias=vcol("b_dw", c)),
                        reads=[psb[slot], vecb], writes=[xb[c]])

            for c in range(KD):
                sv_ = stage("pw1", KD * 128)
                slotv = mm_group([(ring[:, sv_, kc * 128:(kc + 1) * 128], (lambda hh, kc=kc: hb[:, kc, hh * H:(hh + 1) * H])) for kc in range(KD)],
                                 reads=[ringb[sv_]] + hbb)
                sg_ = stage("pw1", KD * 128)
                slotg = mm_group([(ring[:, sg_, kc * 128:(kc + 1) * 128], (lambda hh, kc=kc: hb[:, kc, hh * H:(hh + 1) * H])) for kc in range(KD)],
                                 reads=[ringb[sg_]] + hbb)
                tv, tg = next_tmp(), next_tmp()
                S.op("dve", lambda e, tg=tg, slotg=slotg: e.tensor_tensor(
                    out=halves(tmp_ap(tg)), in0=psv(slotg), in1=halves(rbc[:, :]), op=ALU.mult),
                    reads=[psb[slotg], rbcb], writes=[tmpb[tg]])
                S.op("act", lambda e, tg=tg, c=c: e.activation(
                    out=tmp_ap(tg), in_=tmp_ap(tg), func=AF.Sigmoid, bias=vcol("b_pw1", KD + c)),
                    reads=[tmpb[tg], vecb], writes=[tmpb[tg]])
                S.op("dve", lambda e, tv=tv, slotv=slotv: e.tensor_tensor(
                    out=halves(tmp_ap(tv)), in0=psv(slotv), in1=halves(rbc[:, :]), op=ALU.mult),
                    reads=[psb[slotv], rbcb], writes=[tmpb[tv]])
                ui = c % 2
                U = Ut[ui]
                if p == 1:
                    S.op("act", lambda e, c=c, U=U: e.copy(out=U[:, regs[0][0]:regs[0][0] + HC], in_=uhB[:, c, :]),
                         reads=[uhBb[c]], writes=[Ub[ui]])
                    S.op("act", lambda e, c=c, U=U: e.copy(out=U[:, regs[1][0]:regs[1][0] + HC], in_=uhC[:, c, :]),
                         reads=[uhCb[c]], writes=[Ub[ui]])
                for (ro_, s0, n) in regs:
                    S.op("dve", lambda e, U=U, ro_=ro_, s0=s0, n=n, tv=tv, tg=tg, c=c: e.scalar_tensor_tensor(
                        out=U[:, ro_ + HC:ro_ + HC + n], in0=tmp_ap(tv)[:, s0:s0 + n], scalar=vcol("b_pw1", c),
                        in1=tmp_ap(tg)[:, s0:s0 + n], op0=ALU.add, op1=ALU.mult),
                        reads=[tmpb[tv], tmpb[tg], Ub[ui], vecb], writes=[Ub[ui]])
                if p == 0:
                    mo_ = cfg.voff["mask"]
                    S.op("dve", lambda e, U=U: e.tensor_scalar(out=U[:, HC:HC + HALO], in0=U[:, HC:HC + HALO],
                                                              scalar1=vecs[:, mo_:mo_ + 1], scalar2=None, op0=ALU.mult),
                         reads=[Ub[ui], vecb], writes=[Ub[ui]])
                    a = HC + cfg.real_end0 - HC
                    S.op("act", lambda e, c=c, U=U, a=a: e.copy(out=uhB[:, c, :], in_=U[:, a:a + HC]),
                         reads=[Ub[ui]], writes=[uhBb[c]])
                else:
                    a = regs[0][0] + HC + regs[0][2] - HC
                    S.op("act", lambda e, c=c, U=U, a=a: e.copy(out=uhB[:, c, :], in_=U[:, a:a + HC]),
                         reads=[Ub[ui]], writes=[uhBb[c]])
                    a2 = regs[1][0] + HC + regs[1][2] - HC
                    S.op("act", lambda e, c=c, U=U, a2=a2: e.copy(out=uhC[:, c, :], in_=U[:, a2:a2 + HC]),
                         reads=[Ub[ui]], writes=[uhCb[c]])
                S.op("act", lambda e, U=U, ui=ui: e.copy(out=Ubft[ui][:, 0:NE], in_=U[:, 0:NE]),
                     reads=[Ub[ui]], writes=[Ubfb[ui]])
                if c >= 1:
                    dwconv(c - 1)
            dwconv(KD - 1)
            if p == 1:
                S.op("sp", lambda e: e.dma_start(out=conv_out[0].rearrange("(c q) j -> q c j", q=128), in_=uhB[:, :, :]),
                     reads=uhBb, sem="outs", inc=16)
                S.op("sp", lambda e: e.dma_start(out=conv_out[1].rearrange("(c q) j -> q c j", q=128), in_=uhC[:, :, :]),
                     reads=uhCb, sem="outs", inc=16)
            s1, s2 = next_slot(), next_slot()
            acc1 = scr[:, mo + 2 * T:mo + 3 * T]
            acc2 = scr[:, mo + 3 * T:mo + 4 * T]
            acc1b, acc2b = Buf(), Buf()
            for c in range(KD):
                if c == 0:
                    S.op("act", lambda e: e.activation(out=acc2, in_=xT[:, 0, :], func=AF.Square), reads=[xb[0]], writes=[acc2b])
                    S.op("act", lambda e: e.copy(out=acc1, in_=xT[:, 0, :]), reads=[xb[0]], writes=[acc1b])
                else:
                    ti = next_tmp()
                    S.op("act", lambda e, c=c, ti=ti: e.activation(out=tmp_ap(ti), in_=xT[:, c, :], func=AF.Square),
                         reads=[xb[c]], writes=[tmpb[ti]])
                    S.op("dve", lambda e, c=c: e.tensor_tensor(out=acc1, in0=acc1, in1=xT[:, c, :], op=ALU.add),
                         reads=[xb[c], acc1b], writes=[acc1b])
                    S.op("dve", lambda e, ti=ti: e.tensor_tensor(out=acc2, in0=acc2, in1=tmp_ap(ti), op=ALU.add),
                         reads=[tmpb[ti], acc2b], writes=[acc2b])

            def fn(e):
                ins = None
                for hh in range(2):
                    e.matmul(ps[:, s1, hh, 0:H], lhsT=ones[:, :], rhs=acc1[:, hh * H:(hh + 1) * H], start=True, stop=True)
                    ins = e.matmul(ps[:, s2, hh, 0:H], lhsT=ones[:, :], rhs=acc2[:, hh * H:(hh + 1) * H], start=True, stop=True)
                return ins
            S.op("pe", fn, reads=[acc1b, acc2b, onesb], writes=[psb[s1], psb[s2]])
            S.op("act", lambda e: e.activation(out=halves(mean_t), in_=psv(s1), func=AF.Identity, scale=1.0 / D),
                 reads=[psb[s1]], writes=[meanb])
            tq = next_tmp()
            S.op("dve", lambda e: e.tensor_tensor(out=tmp_ap(tq), in0=mean_t, in1=mean_t, op=ALU.mult),
                 reads=[meanb], writes=[tmpb[tq]])
            S.op("dve", lambda e: e.scalar_tensor_tensor(out=halves(rstd_t), in0=psv(s2), scalar=1.0 / D, in1=halves(tmp_ap(tq)),
                                                         op0=ALU.mult, op1=ALU.subtract),
                 reads=[psb[s2], tmpb[tq]], writes=[rstdb])
            S.op("act", lambda e: e.activation(out=rstd_t, in_=rstd_t, func=AF.Sqrt, bias=EPS),
                 reads=[rstdb], writes=[rstdb])
            S.op("dve", lambda e: e.reciprocal(out=rstd_t, in_=rstd_t), reads=[rstdb], writes=[rstdb])
            for c in range(KD):
                t1, t2 = next_tmp(), next_tmp()
                S.op("dve", lambda e, c=c, t1=t1: e.tensor_tensor(out=tmp_ap(t1), in0=xT[:, c, :], in1=mean_t, op=ALU.subtract),
                     reads=[xb[c], meanb], writes=[tmpb[t1]])
                S.op("dve", lambda e, t1=t1, t2=t2: e.tensor_tensor(out=tmp_ap(t2), in0=tmp_ap(t1), in1=rstd_t, op=ALU.mult),
                     reads=[tmpb[t1], rstdb], writes=[tmpb[t2]])
                S.op("act", lambda e, c=c, t2=t2: e.activation(out=hb[:, c, :], in_=tmp_ap(t2), func=AF.Silu,
                                                              bias=vcol("ln_b", c), scale=vcol("ln_g", c)),
                     reads=[tmpb[t2], vecb], writes=[hbb[c]])
            set_next("g_ffn2", 1, inline=False)
            S.op("sp", lambda e: e.dma_start(out=xT[:, :, :], in_=xspill.rearrange("(c q) t -> q c t", q=128)),
                 reads=[spillb], writes=xb, sem="spill", inc=16)
            for oc in range(KD):
                s_ = stage("pw2", KD * 128)
                slot = mm_group([(ring[:, s_, kc * 128:(kc + 1) * 128], (lambda hh, kc=kc: hb[:, kc, hh * H:(hh + 1) * H])) for kc in range(KD)],
                                reads=[ringb[s_]] + hbb)
                S.op("dve", lambda e, oc=oc, slot=slot: e.scalar_tensor_tensor(
                    out=halves(xT[:, oc, :]), in0=psv(slot), scalar=vcol("b_pw2", oc), in1=halves(xT[:, oc, :]),
                    op0=ALU.add, op1=ALU.add),
                    reads=[psb[slot], xb[oc], vecb], writes=[xb[oc]])
                post_x(oc)
            norm_close()
            S.set_fence()

        def ple(p, l):
            S.op("pool", lambda e: e.dma_start(out=pT[:, :, :], in_=pin[p * 2 + l].rearrange("(k q) t -> q k t", q=128)),
                 writes=[pTb], sem="pt", inc=16)
            for c in range(KD):
                s_ = stage("ple", SW)
                slotg = mm_group([(ring[:, s_, kc * 128:(kc + 1) * 128], (lambda hh, kc=kc: hb[:, kc, hh * H:(hh + 1) * H])) for kc in range(KD)],
                                 reads=[ringb[s_]] + hbb)
                slotp = mm_group([(ring[:, s_, (KD + kc) * 128:(KD + kc + 1) * 128], (lambda hh, kc=kc: pT[:, kc, hh * H:(hh + 1) * H])) for kc in range(KP)],
                                 reads=[ringb[s_], pTb])
                tg, tm = next_tmp(), next_tmp()
                S.op("dve", lambda e, tg=tg, slotg=slotg: e.tensor_tensor(
                    out=halves(tmp_ap(tg)), in0=psv(slotg), in1=halves(rbc[:, :]), op=ALU.mult),
                    reads=[psb[slotg], rbcb], writes=[tmpb[tg]])
                S.op("act", lambda e, tg=tg: e.activation(out=tmp_ap(tg), in_=tmp_ap(tg), func=AF.Sigmoid),
                     reads=[tmpb[tg]], writes=[tmpb[tg]])
                S.op("dve", lambda e, tg=tg, tm=tm, slotp=slotp: e.tensor_tensor(
                    out=halves(tmp_ap(tm)), in0=psv(slotp), in1=halves(tmp_ap(tg)), op=ALU.mult),
                    reads=[psb[slotp], tmpb[tg]], writes=[tmpb[tm]])
                S.op("dve", lambda e, c=c, tm=tm: e.tensor_tensor(out=xT[:, c, :], in0=xT[:, c, :], in1=tmp_ap(tm), op=ALU.add),
                     reads=[tmpb[tm], xb[c]], writes=[xb[c]])
                post_x(c)
            norm_close()

        for p in range(2):
            S.op("sp", lambda e, p=p: e.dma_start(out=xT[:, :, :], in_=xin[p].rearrange("(c q) t -> q c t", q=128)),
                 writes=xb, sem="x", inc=16)
            set_next("g_ffn1", 0, inline=True)
            for c in range(KD):
                post_x(c)
            norm_close()
            for l in range(2):
                if l == 0:
                    set_next(None, inline=True)
                else:
                    set_next("g_mix", 1, inline=True)
                ffn(l, 1)
                if l == 0:
                    pool_mixer(p)
                else:
                    conv_mixer(p)
                set_next("g_ple", l, inline=True)
                ffn(l, 2)
                if l == 0:
                    set_next("g_ffn1", 1, inline=False)
                else:
                    set_next(None, inline=False)
                ple(p, l)
            for c in range(KD):
                S.op("dve", lambda e, c=c: e.scalar_tensor_tensor(out=xT[:, c, :], in0=xT[:, c, :], scalar=vcol("g_final", c),
                                                                  in1=rbc[:, :], op0=ALU.mult, op1=ALU.mult),
                     reads=[xb[c], rbcb, vecb], writes=[xb[c]])
            S.op("sp", lambda e, p=p: e.dma_start(out=yout[p].rearrange("(c q) t -> q c t", q=128), in_=xT[:, :, :]),
                 reads=xb, sem="outy", inc=16)
        assert st["k"] == 2 * NS
        final_waits = [(k, S.count[k]) for k in ("outy", "outs", "spill") if k in S.count]

        with nc.Block() as block:
            @block.tensor
            def _(e):
                S.emit("pe", e, sems)

            @block.scalar
            def _(e):
                S.emit("act", e, sems)

            @block.vector
            def _(e):
                S.emit("dve", e, sems)

            @block.gpsimd
            def _(e):
                S.emit("pool", e, sems)

            @block.sync
            def _(e):
                S.emit("sp", e, sems)
                for k, v in final_waits:
                    e.wait_ge(sems[k], v)
    return nc


def _tile_w(W, KD):
    K, N = W.shape
    return np.ascontiguousarray(W.reshape(KD, 128, N // 128, 128).transpose(2, 1, 0, 3)).reshape(N // 128, 128, KD * 128)


def _vec_cols(v):
    v = np.asarray(v, np.float32).reshape(-1)
    return np.ascontiguousarray(v.reshape(-1, 128).T)


def build_w_all(cfg, inp):
    plan = stage_plan(cfg)
    NS = len(plan)
    KD, KF, KP, KG = cfg.KD, cfg.KF, cfg.KP, cfg.KG
    W = np.zeros((NS, 128, cfg.SW), np.float32)
    idx = {}
    for k, e in enumerate(plan):
        if e[0] in ("g", "u", "d"):
            key = (e[0], e[1], e[2])
        elif e[0] == "ple":
            key = ("ple", e[1])
        else:
            key = (e[0],)
        idx.setdefault(key, []).append(k)
    for l in range(2):
        for which in (1, 2):
            names = {1: ("ffn1_w_gate", "ffn1_w_up", "ffn1_w_down"), 2: ("ffn2_w_gate", "ffn2_w_up", "ffn2_w_down")}[which]
            wg, wu, wd = inp[names[0]][l], inp[names[1]][l], inp[names[2]][l]
            W[idx[("g", l, which)], :, :KD * 128] = _tile_w(wg, KD)
            W[idx[("u", l, which)], :, :KD * 128] = _tile_w(wu, KD)
            W[idx[("d", l, which)], :, :cfg.D] = wd.reshape(KF, 128, cfg.D)
    pk = idx[("pool",)]
    pw = inp["pool_w"][0]
    for g in range(4):
        W[pk[g * KG:(g + 1) * KG], :, :KG * 128] = _tile_w(pw[g], KG)
    t1 = _tile_w(inp["conv_w_pw1"][0], KD)
    order = []
    for c in range(KD):
        order += [c, KD + c]
    W[idx[("pw1",)], :, :KD * 128] = t1[order]
    W[idx[("pw2",)], :, :KD * 128] = _tile_w(inp["conv_w_pw2"][0], KD)
    wdw = inp["conv_w_dw"][0].reshape(CW, KD, 128)
    dwt = np.zeros((KD, 128, CW * 128), np.float32)
    qq = np.arange(128)
    for k in range(CW):
        dwt[:, qq, k * 128 + qq] = wdw[k]
    W[idx[("dw",)], :, :CW * 128] = dwt
    for l in range(2):
        kk = idx[("ple", l)]
        W[kk, :, :KD * 128] = _tile_w(inp["ple_w_gate"][l], KD)
        W[kk, :, KD * 128:(KD + KP) * 128] = _tile_w(inp["ple_w_proj"][l], KP)
    return W


def prep_inputs(cfg, inp):
    D, T, KD = cfg.D, cfg.T, cfg.KD
    inp = {k: np.asarray(v) for k, v in inp.items()}
    w_all = build_w_all(cfg, inp)
    vecs = np.zeros((128, cfg.NV), np.float32)
    for l in range(2):
        for nm in ("g_ffn1", "g_mix", "g_ffn2", "g_ple"):
            o = cfg.voff[(nm, l)]
            vecs[:, o:o + KD] = _vec_cols(inp[nm][l])
    for nm, src in (("g_final", inp["g_final"]), ("pool_b", inp["pool_b"][0]), ("pool_scale", inp["pool_scale"][0]),
                    ("b_pw1", inp["conv_b_pw1"][0]), ("b_dw", inp["conv_b_dw"][0]), ("ln_g", inp["conv_ln_g"][0]),
                    ("ln_b", inp["conv_ln_b"][0]), ("b_pw2", inp["conv_b_pw2"][0])):
        o = cfg.voff[nm]
        vc = _vec_cols(src)
        vecs[:, o:o + vc.shape[1]] = vc
    xp, xs = inp["x_prompt"], inp["x_sample"]
    pp, psm = inp["p_prompt"], inp["p_sample"]
    in_maps = []
    for c in range(N_CORES):
        b, q = c // cfg.CPB, c % cfg.CPB
        s = q * cfg.NP
        xin = np.zeros((2, T, D), np.float32)
        pin = np.zeros((2, 2, T, cfg.PLE), np.float32)
        lo = s - HALO
        src_lo = max(lo, 0)
        n_real = s + cfg.OWN1 - src_lo
        xin[0, src_lo - lo:src_lo - lo + n_real] = xp[b, src_lo:s + cfg.OWN1]
        pin[0, :, src_lo - lo:src_lo - lo + n_real] = pp[:, b, src_lo:s + cfg.OWN1]
        xin[1, :cfg.OWN2] = xp[b, s + cfg.OWN1:s + cfg.NP]
        pin[1, :, :cfg.OWN2] = pp[:, b, s + cfg.OWN1:s + cfg.NP]
        xin[1, cfg.OWN2:] = xs[c]
        pin[1, :, cfg.OWN2:] = psm[:, c]
        pos = np.zeros((2, T), np.int64)
        pos[0] = lo + np.arange(T)
        pos[1, :cfg.OWN2] = s + cfg.OWN1 + np.arange(cfg.OWN2)
        pos[1, cfg.OWN2:] = cfg.PAST_LEN + np.arange(cfg.DEC_SEQ)
        inv = np.zeros((2, 4, T), np.float32)
        for g, win in enumerate(POOL_WINDOWS):
            cnt = np.minimum(np.maximum(pos, 0) + 1, win).astype(np.float32)
            inv[:, g, :] = (np.float32(1.0) / cnt).astype(np.float32)
        invc = np.ascontiguousarray(np.broadcast_to(inv.reshape(2, 1, 4 * T), (2, 128, 4 * T)))
        vc = vecs.copy()
        vc[:, cfg.voff["mask"]] = 0.0 if q == 0 else 1.0
        in_maps.append({
            "xin": np.ascontiguousarray(xin.transpose(0, 2, 1)),
            "pin": np.ascontiguousarray(pin.transpose(0, 1, 3, 2)).reshape(4, cfg.PLE, T),
            "invc": invc,
            "spool": np.ascontiguousarray(inp["state_pool"][0, c].T),
            "sconv": np.ascontiguousarray(inp["state_conv"][0, c].T),
            "vecs": vc,
            "w_all": w_all,
        })
    return in_maps


def assemble(cfg, results):
    D = cfg.D
    y_prompt = np.zeros((cfg.BATCH, cfg.SEQ, D), np.float32)
    y_sample = np.zeros((cfg.DEC_BATCH, cfg.DEC_SEQ, D), np.float32)
    npp = np.zeros((1, cfg.BATCH, HP, D), np.float32)
    ncp = np.zeros((1, cfg.BATCH, HC, D), np.float32)
    nps = np.zeros((1, cfg.DEC_BATCH, HP, D), np.float32)
    ncs = np.zeros((1, cfg.DEC_BATCH, HC, D), np.float32)
    for c in range(N_CORES):
        r = results[c]
        b, q = c // cfg.CPB, c % cfg.CPB
        s = q * cfg.NP
        yo = r["yout"]
        y_prompt[b, s:s + cfg.OWN1] = yo[0][:, HALO:HALO + cfg.OWN1].T
        y_prompt[b, s + cfg.OWN1:s + cfg.NP] = yo[1][:, :cfg.OWN2].T
        y_sample[c] = yo[1][:, cfg.OWN2:].T
        nps[0, c] = r["pool_out"][1].T
        ncs[0, c] = r["conv_out"][1].T
        if q == cfg.CPB - 1:
            npp[0, b] = r["pool_out"][0].T
            ncp[0, b] = r["conv_out"][0].T
    return (y_prompt, y_sample, npp, ncp, nps, ncs)


def run(cfg, inputs, trace=False):
    nc = build_program(cfg)
    in_maps = prep_inputs(cfg, inputs)
    res = run_bass_kernel_spmd(nc, in_maps, core_ids=list(range(N_CORES)), trace=trace)
    return assemble(cfg, res.results), res


def kernel(**inputs):
    cfg = Cfg()
    outs, _ = run(cfg, inputs)
    return outs
```
